# Optimizing a Trainium2 kernel written in Bass

```python
import math
import jax, jax.numpy as jnp
from jax import lax
import numpy as np

D_MODEL = 2048
BATCH = 4
SEQ = 2048
DEPTH = 2
DEC_BATCH = 8
DEC_SEQ = 32
PAST_LEN = 1024

CHUNK = 64
Q_BLOCK = 128
FOX_HEADS = 8
FOX_DH = 128
FOX_W = FOX_HEADS * FOX_DH
ATTN_SCALE = FOX_DH ** -0.5
GM_GROUPS = 4
GM_CH = 128
GM_W = GM_GROUPS * GM_CH
GM_CHUNK = 128
SSD_HEADS = 8
SSD_P = 64
SSD_W = SSD_HEADS * SSD_P
SSD_N = 128
SSD_G = 2
SSD_CONV = 4
SSD_CONV_CH = SSD_W + 2 * SSD_G * SSD_N
D_MIX = FOX_W + GM_W + SSD_W
D_FF = 5632
N_IN = 3 * FOX_W + FOX_HEADS + 2 * GM_W + SSD_W + SSD_CONV_CH + SSD_HEADS
EPS = 1e-6

kernel_name = 'hybrid_fox_gmlp_ssd_stream_step'


def rmsnorm(x, w):
    xf = x.astype(jnp.float32)
    y = xf * lax.rsqrt(jnp.mean(xf * xf, axis=-1, keepdims=True) + EPS)
    return (y * w.astype(jnp.float32)).astype(x.dtype)


def layernorm(x, w, b):
    xf = x.astype(jnp.float32)
    mu = jnp.mean(xf, axis=-1, keepdims=True)
    var = jnp.mean(jnp.square(xf - mu), axis=-1, keepdims=True)
    y = (xf - mu) * lax.rsqrt(var + EPS)
    return (y * w.astype(jnp.float32) + b.astype(jnp.float32)).astype(x.dtype)


def swiglu(h, wg, wu, wd):
    a = jnp.einsum('bld,df->blf', h, wg)
    g = jnp.einsum('bld,df->blf', h, wu)
    return jnp.einsum('blf,fd->bld', jax.nn.silu(a) * g, wd)


def split_in(h, w_in):
    proj = jnp.einsum('bld,dn->bln', h, w_in)
    sizes = [FOX_W, FOX_W, FOX_W, FOX_HEADS, GM_W, GM_W, SSD_W, SSD_CONV_CH, SSD_HEADS]
    return jnp.split(proj, np.cumsum(sizes)[:-1].tolist(), axis=-1)


def fox_prompt(q, k, v, logf):
    bsz, s_len = q.shape[0], q.shape[1]
    c = jnp.cumsum(logf, axis=1)
    c_key = jnp.transpose(c, (0, 2, 1))[:, :, None, :]
    k_pos = jnp.arange(s_len)

    def one_block(i):
        start = i * Q_BLOCK
        qb = lax.dynamic_slice_in_dim(q, start, Q_BLOCK, axis=1)
        cb = jnp.transpose(lax.dynamic_slice_in_dim(c, start, Q_BLOCK, axis=1), (0, 2, 1))[..., None]
        s = jnp.einsum('bqhd,bkhd->bhqk', qb, k).astype(jnp.float32) * ATTN_SCALE + cb - c_key
        q_pos = start + jnp.arange(Q_BLOCK)
        s = jnp.where(k_pos[None, :] <= q_pos[:, None], s, -jnp.inf)
        p = jax.nn.softmax(s, axis=-1).astype(v.dtype)
        return jnp.einsum('bhqk,bkhd->bqhd', p, v)

    o = lax.map(one_block, jnp.arange(s_len // Q_BLOCK))
    return jnp.transpose(o, (1, 0, 2, 3, 4)).reshape(bsz, s_len, FOX_W)


def fox_sample(q, k, v, logf, ck, cv, clogf):
    bsz, L = q.shape[0], q.shape[1]
    P = ck.shape[1]
    keys = jnp.concatenate([ck, k], axis=1)
    vals = jnp.concatenate([cv, v], axis=1)
    c = jnp.cumsum(jnp.concatenate([clogf.astype(jnp.float32), logf], axis=1), axis=1)
    cq = jnp.transpose(c[:, P:], (0, 2, 1))[..., None]
    ckey = jnp.transpose(c, (0, 2, 1))[:, :, None, :]
    s = jnp.einsum('bqhd,bkhd->bhqk', q, keys).astype(jnp.float32) * ATTN_SCALE + cq - ckey
    q_pos = P + jnp.arange(L)
    k_pos = jnp.arange(P + L)
    s = jnp.where(k_pos[None, :] <= q_pos[:, None], s, -jnp.inf)
    p = jax.nn.softmax(s, axis=-1).astype(vals.dtype)
    return jnp.einsum('bhqk,bkhd->bqhd', p, vals).reshape(bsz, L, FOX_W)


def gm_mix(u, v, ws, bs):
    bsz, L = v.shape[0], v.shape[1]
    lc = GM_CHUNK if L % GM_CHUNK == 0 else L
    wm = jnp.tril(ws[:, :lc, :lc])
    vc = v.reshape(bsz, L // lc, lc, GM_GROUPS, GM_CH)
    sp = jnp.einsum('gts,bcsgd->bctgd', wm, vc) + jnp.transpose(bs[:, :lc])[None, None, :, :, None]
    return u * sp.reshape(bsz, L, GM_W).astype(u.dtype)


def ssd_scan(xh, dt, A, Bm, Cm, s0, q_len):
    bsz, L = xh.shape[0], xh.shape[1]
    nc = L // q_len
    x_c = xh.reshape(bsz, nc, q_len, SSD_HEADS, SSD_P)
    dt_c = dt.reshape(bsz, nc, q_len, SSD_HEADS)
    b_c = Bm.reshape(bsz, nc, q_len, SSD_HEADS, SSD_N)
    c_c = Cm.reshape(bsz, nc, q_len, SSD_HEADS, SSD_N)
    acum = jnp.cumsum(dt_c * A, axis=2)
    diff = acum[:, :, :, None, :] - acum[:, :, None, :, :]
    causal = jnp.tril(jnp.ones((q_len, q_len), dtype=bool))
    lmat = jnp.exp(jnp.where(causal[None, None, :, :, None], diff, -jnp.inf))
    scores = jnp.einsum('bcthn,bcshn->bctsh', c_c, b_c) * lmat * dt_c[:, :, None, :, :]
    y_intra = jnp.einsum('bctsh,bcshp->bcthp', scores, x_c)
    decay_end = jnp.exp(acum[:, :, -1:, :] - acum) * dt_c
    chunk_states = jnp.einsum('bcsh,bcshn,bcshp->bchpn', decay_end, b_c, x_c)
    chunk_decay = jnp.exp(acum[:, :, -1, :])

    def step(s, inp):
        st, dec = inp
        return s * dec[:, :, None, None] + st, s

    s_final, s_in = lax.scan(step, s0, (jnp.moveaxis(chunk_states, 1, 0), jnp.moveaxis(chunk_decay, 1, 0)))
    s_in = jnp.moveaxis(s_in, 0, 1)
    y_inter = jnp.einsum('bcthn,bchpn->bcthp', c_c, s_in) * jnp.exp(acum)[..., None]
    return (y_intra + y_inter).reshape(bsz, L, SSD_HEADS, SSD_P), s_final


def ssd_mixer(z, xbc, dtr, conv_prev, s0, conv_w, conv_b, dt_bias, a_log, d_skip, norm_w):
    bsz, L = xbc.shape[0], xbc.shape[1]
    xpad = jnp.concatenate([conv_prev.astype(xbc.dtype), xbc], axis=1)
    acc = conv_b
    for i in range(SSD_CONV):
        acc = acc + xpad[:, i:i + L] * conv_w[i]
    act = jax.nn.silu(acc)
    new_conv = xpad[:, L:]
    xs, bm, cm = jnp.split(act, [SSD_W, SSD_W + SSD_G * SSD_N], axis=-1)
    xh = xs.reshape(bsz, L, SSD_HEADS, SSD_P).astype(jnp.float32)
    rep = SSD_HEADS // SSD_G
    bm = jnp.repeat(bm.reshape(bsz, L, SSD_G, SSD_N), rep, axis=2).astype(jnp.float32)
    cm = jnp.repeat(cm.reshape(bsz, L, SSD_G, SSD_N), rep, axis=2).astype(jnp.float32)
    dt = jax.nn.softplus(dtr.astype(jnp.float32) + dt_bias.astype(jnp.float32))
    A = -jnp.exp(a_log.astype(jnp.float32))
    q_len = CHUNK if L % CHUNK == 0 else L
    y, s_final = ssd_scan(xh, dt, A, bm, cm, s0.astype(jnp.float32), q_len)
    y = (y + d_skip.astype(jnp.float32)[:, None] * xh).reshape(bsz, L, SSD_W)
    y = rmsnorm(y * jax.nn.silu(z.astype(jnp.float32)), norm_w)
    return y.astype(z.dtype), new_conv, s_final.astype(z.dtype)


def mixer(h, l, p, cache):
    q, k, v, f_raw, gu, gv, z, xbc, dtr = split_in(h, p['w_in'][l])
    bsz, L = h.shape[0], h.shape[1]
    q = q.reshape(bsz, L, FOX_HEADS, FOX_DH)
    k = k.reshape(bsz, L, FOX_HEADS, FOX_DH)
    v = v.reshape(bsz, L, FOX_HEADS, FOX_DH)
    logf = jax.nn.log_sigmoid(f_raw.astype(jnp.float32) + p['fox_fb'][l].astype(jnp.float32))
    u = jax.nn.gelu(gu)
    gm_v = layernorm(jax.nn.gelu(gv), p['gm_ln_w'][l], p['gm_ln_b'][l])
    gm_o = gm_mix(u, gm_v, p['gm_ws'][l], p['gm_bs'][l])
    if cache is None:
        fox_o = fox_prompt(q, k, v, logf)
        conv_prev = jnp.zeros((bsz, SSD_CONV - 1, SSD_CONV_CH), xbc.dtype)
        ssm_prev = jnp.zeros((bsz, SSD_HEADS, SSD_P, SSD_N), jnp.float32)
    else:
        ck, cv, clf, cconv, cssm = cache
        fox_o = fox_sample(q, k, v, logf, ck[l], cv[l], clf[l])
        conv_prev, ssm_prev = cconv[l], cssm[l]
    ssd_o, new_conv, new_ssm = ssd_mixer(z, xbc, dtr, conv_prev, ssm_prev, p['ssd_conv_w'][l], p['ssd_conv_b'][l],
                                         p['ssd_dt_bias'][l], p['ssd_a_log'][l], p['ssd_d'][l], p['ssd_norm_w'][l])
    mixed = jnp.concatenate([fox_o, gm_o, ssd_o], axis=-1)
    o = jnp.einsum('bln,nd->bld', mixed, p['w_out'][l])
    if cache is None:
        return o, (k, v, logf, new_conv, new_ssm)
    return o, (k, v, logf, gm_v, new_conv, new_ssm)


def run_trunk(x, p, cache):
    states = []
    for l in range(DEPTH):
        nw = p['norm_w'][l]
        h = rmsnorm(x, nw[0])
        x = x + 0.5 * rmsnorm(swiglu(h, p['w_ffn_gate'][l, 0], p['w_ffn_up'][l, 0], p['w_ffn_down'][l, 0]), nw[1])
        o, st = mixer(rmsnorm(x, nw[2]), l, p, cache)
        x = x + rmsnorm(o, nw[3])
        h = rmsnorm(x, nw[4])
        x = x + 0.5 * rmsnorm(swiglu(h, p['w_ffn_gate'][l, 1], p['w_ffn_up'][l, 1], p['w_ffn_down'][l, 1]), nw[5])
        states.append(st)
    return x, [jnp.stack(t) for t in zip(*states)]


def setup_inputs(seed: int = 0) -> dict:
    key = jax.random.key(seed)
    ks = jax.random.split(key, 24)
    f32 = jnp.float32

    def nrm(k, shape, scale):
        return scale * jax.random.normal(k, shape, f32)

    dt0 = jnp.exp(jax.random.uniform(ks[19], (DEPTH, SSD_HEADS), f32, math.log(1e-3), math.log(1e-1)))
    return {
        'x_prompt': nrm(ks[0], (BATCH, SEQ, D_MODEL), 1.0),
        'x_sample': nrm(ks[1], (DEC_BATCH, DEC_SEQ, D_MODEL), 1.0),
        'cache_fox_k': nrm(ks[2], (DEPTH, DEC_BATCH, PAST_LEN, FOX_HEADS, FOX_DH), 1.0),
        'cache_fox_v': nrm(ks[3], (DEPTH, DEC_BATCH, PAST_LEN, FOX_HEADS, FOX_DH), 1.0),
        'cache_fox_logf': jax.nn.log_sigmoid(1.0 + nrm(ks[4], (DEPTH, DEC_BATCH, PAST_LEN, FOX_HEADS), 1.0)),
        'state_ssd_conv': nrm(ks[5], (DEPTH, DEC_BATCH, SSD_CONV - 1, SSD_CONV_CH), 1.0),
        'state_ssd': nrm(ks[6], (DEPTH, DEC_BATCH, SSD_HEADS, SSD_P, SSD_N), 0.5),
        'norm_w': 1.0 + nrm(ks[7], (DEPTH, 6, D_MODEL), 0.05),
        'w_ffn_gate': nrm(ks[8], (DEPTH, 2, D_MODEL, D_FF), D_MODEL ** -0.5),
        'w_ffn_up': nrm(ks[9], (DEPTH, 2, D_MODEL, D_FF), D_MODEL ** -0.5),
        'w_ffn_down': nrm(ks[10], (DEPTH, 2, D_FF, D_MODEL), D_FF ** -0.5),
        'w_in': nrm(ks[11], (DEPTH, D_MODEL, N_IN), D_MODEL ** -0.5),
        'fox_fb': 1.0 + nrm(ks[12], (DEPTH, FOX_HEADS), 0.5),
        'gm_ln_w': 1.0 + nrm(ks[13], (DEPTH, GM_W), 0.05),
        'gm_ln_b': nrm(ks[14], (DEPTH, GM_W), 0.02),
        'gm_ws': nrm(ks[15], (DEPTH, GM_GROUPS, GM_CHUNK, GM_CHUNK), GM_CHUNK ** -0.5),
        'gm_bs': 1.0 + nrm(ks[16], (DEPTH, GM_GROUPS, GM_CHUNK), 0.1),
        'ssd_conv_w': nrm(ks[17], (DEPTH, SSD_CONV, SSD_CONV_CH), SSD_CONV ** -0.5),
        'ssd_conv_b': nrm(ks[18], (DEPTH, SSD_CONV_CH), 0.02),
        'ssd_dt_bias': dt0 + jnp.log(-jnp.expm1(-dt0)),
        'ssd_a_log': jnp.log(jax.random.uniform(ks[20], (DEPTH, SSD_HEADS), f32, 1.0, 16.0)),
        'ssd_d': 1.0 + nrm(ks[21], (DEPTH, SSD_HEADS), 0.1),
        'ssd_norm_w': 1.0 + nrm(ks[22], (DEPTH, SSD_W), 0.05),
        'w_out': nrm(ks[23], (DEPTH, D_MIX, D_MODEL), D_MIX ** -0.5),
    }


def reference(x_prompt, x_sample, cache_fox_k, cache_fox_v, cache_fox_logf, state_ssd_conv, state_ssd,
              norm_w, w_ffn_gate, w_ffn_up, w_ffn_down, w_in, fox_fb, gm_ln_w, gm_ln_b, gm_ws, gm_bs,
              ssd_conv_w, ssd_conv_b, ssd_dt_bias, ssd_a_log, ssd_d, ssd_norm_w, w_out):
    p = {'norm_w': norm_w, 'w_ffn_gate': w_ffn_gate, 'w_ffn_up': w_ffn_up, 'w_ffn_down': w_ffn_down,
         'w_in': w_in, 'fox_fb': fox_fb, 'gm_ln_w': gm_ln_w, 'gm_ln_b': gm_ln_b, 'gm_ws': gm_ws, 'gm_bs': gm_bs,
         'ssd_conv_w': ssd_conv_w, 'ssd_conv_b': ssd_conv_b, 'ssd_dt_bias': ssd_dt_bias, 'ssd_a_log': ssd_a_log,
         'ssd_d': ssd_d, 'ssd_norm_w': ssd_norm_w, 'w_out': w_out}
    y_prompt, (p_k, p_v, p_lf, p_conv, p_ssm) = run_trunk(x_prompt, p, None)
    y_sample, (s_k, s_v, s_lf, s_gm, s_conv, s_ssm) = run_trunk(
        x_sample, p, (cache_fox_k, cache_fox_v, cache_fox_logf, state_ssd_conv, state_ssd))
    return (y_prompt, y_sample, p_k, p_v, p_lf, p_conv, p_ssm, s_k, s_v, s_lf, s_gm, s_conv, s_ssm)
```

```python
from contextlib import ExitStack
import numpy as np
import concourse.bass as bass
import concourse.mybir as mybir
from concourse.bass_utils import run_bass_kernel_spmd

F32, BF16 = mybir.dt.float32, mybir.dt.bfloat16
AF = mybir.ActivationFunctionType
ALU = mybir.AluOpType
AX = mybir.AxisListType
CELL = 512
SELF_WAIT = {'pe': False, 'act': True, 'dve': True, 'pool': True, 'sp': False}


class Sem:
    def __init__(self, h, name, is_dma):
        self.h = h
        self.name = name
        self.is_dma = is_dma
        self.count = 0


class Unit:
    def __init__(self, B, off, nbytes, dtype=F32):
        assert off % CELL == 0 and nbytes % 4 == 0, (off, nbytes)
        assert off + nbytes <= B.arena_bytes, (off, nbytes, B.arena_bytes)
        self.off, self.nbytes = off, nbytes
        self.cells = [('sb', i) for i in range(off // CELL, (off + nbytes + CELL - 1) // CELL)]
        ap = B.arena[:, off // 4:(off + nbytes) // 4]
        if dtype == BF16:
            ap = ap.bitcast(BF16)
        self.ap = ap


class Builder:
    def __init__(self, nc, arena, arena_bytes, psum, stack):
        self.nc = nc
        self.arena = arena
        self.arena_bytes = arena_bytes
        self.psum = psum
        self.stack = stack
        self.q = {e: [] for e in ('pe', 'act', 'dve', 'pool', 'sp')}
        self.seen = {e: {} for e in self.q}
        self.cells = {}
        self.esem = {e: self.new_sem('e_' + e, False) for e in ('pe', 'act', 'dve', 'pool')}
        self.dsems = []
        self.ps_rr = 0
        self.dry = False
        self.wreq = []
        self.wi = 0
        self.wissued = 0
        self.wslots = None
        self.nops = {e: 0 for e in self.q}

    def new_sem(self, name, is_dma=True):
        h = self.stack.enter_context(self.nc.semaphore(name))
        s = Sem(h, name, is_dma)
        if is_dma:
            self.dsems.append(s)
        return s

    def _cells(self, res):
        out = []
        for r in res:
            if isinstance(r, Unit):
                out.extend(r.cells)
            elif isinstance(r, (list, tuple)) and r and isinstance(r[0], Unit):
                for u in r:
                    out.extend(u.cells)
            else:
                out.append(r)
        return out

    def _deps(self, eng, reads, writes):
        need = {}

        def add(ev):
            sem, v = ev
            if sem.is_dma:
                v = sem.count
            elif sem is self.esem.get(eng) and not SELF_WAIT[eng]:
                return
            if need.get(sem, 0) < v:
                need[sem] = v
        for c in reads:
            cell = self.cells.get(c)
            if cell and cell[0]:
                add(cell[0])
            if cell and isinstance(c, tuple) and c[0] == 'ps':
                for ev in cell[1].values():
                    if ev[0] is not self.esem.get(eng):
                        add(ev)
        for c in writes:
            cell = self.cells.get(c)
            if cell:
                if cell[0]:
                    add(cell[0])
                for ev in cell[1].values():
                    add(ev)
        waits = []
        seen = self.seen[eng]
        for sem, v in need.items():
            if seen.get(sem, 0) < v:
                seen[sem] = v
                waits.append((sem, v))
        return waits

    def _commit(self, ev, reads, writes):
        sem, v = ev
        for c in reads:
            cell = self.cells.setdefault(c, [None, {}])
            old = cell[1].get(sem)
            if old is None or old[1] < v:
                cell[1][sem] = ev
        for c in writes:
            self.cells[c] = [ev, {}]

    def op(self, eng, fn, reads=(), writes=()):
        if self.dry:
            return
        reads = self._cells(reads)
        writes = self._cells(writes)
        waits = self._deps(eng, reads, writes)
        sem = self.esem[eng]
        sem.count += 1
        self.q[eng].append((waits, fn, sem, 1))
        self._commit((sem, sem.count), reads, writes)
        self.nops[eng] += 1

    def mm(self, out, pairs, reads, writes):
        if self.dry:
            return
        pairs = list(pairs)

        def fn(t):
            n = len(pairs)
            ins = None
            for i, (l, r) in enumerate(pairs):
                ins = t.matmul(out, l, r, start=(i == 0), stop=(i == n - 1))
            return ins
        self.op('pe', fn, reads, writes)

    def transpose(self, out, in_, ident, reads, writes):
        self.op('pe', lambda t: t.transpose(out, in_, ident), reads, writes)

    def dma(self, q, out, in_, reads, writes, sem):
        if self.dry:
            return
        reads = self._cells(reads)
        writes = self._cells(writes)
        waits = self._deps(q, reads, writes)
        if sem.count and self.seen[q].get(sem, 0) < sem.count:
            self.seen[q][sem] = sem.count
            waits = [w for w in waits if w[0] is not sem] + [(sem, sem.count)]
        sem.count += 16
        self.q[q].append((waits, lambda e: e.dma_start(out=out, in_=in_), sem, 16))
        self._commit((sem, sem.count), reads, writes)
        self.nops[q] += 1

    def dump(self, name, ap, reads):
        if self.dry or not getattr(self, 'debug', False):
            return
        shape = list(ap.shape)
        t = self.nc.dram_tensor("dbg_" + name, shape, F32, kind="ExternalOutput").ap()
        if not hasattr(self, 'dbg_names'):
            self.dbg_names = []
        self.dbg_names.append("dbg_" + name)
        self.dma('pool', t, ap, reads, [], self.new_sem("dbg_" + name))

    def ps(self):
        nres = getattr(self, 'ps_nres', 0)
        nfree = len(self.psum) - nres
        i = nres + (self.ps_rr % nfree)
        self.ps_rr += 1
        return self.psum[i][:, :], ('ps', i)

    def ps_reserve(self, n):
        self.ps_nres = n
        return [(self.psum[i][:, :], ('ps', i)) for i in range(n)]

    def ps_release(self):
        self.ps_nres = 0

    def weights(self, loader):
        if self.dry:
            self.wreq.append(loader)
            return self.wslots[0][0]
        ns = len(self.wslots)
        k = self.wi
        self.wi += 1
        while self.wissued < min(len(self.wreq), k + ns - 1):
            j = self.wissued
            slot, sem = self.wslots[j % ns]
            wc = getattr(self, 'wcache', None)
            if wc is None:
                self.wreq[j](slot, sem)
            else:
                R, n_write, scratch, wb_sems = wc
                p, r = j // R, j % R
                if p >= n_write:
                    self.dma('pool', slot.ap, scratch[r], [('wsc', r)], [slot], sem)
                else:
                    self.wreq[j](slot, sem)
                    if r % n_write == p:
                        self.dma('sp', scratch[r], slot.ap, [slot], [('wsc', r)], wb_sems[j % ns])
            self.wissued += 1
        return self.wslots[k % ns][0]

    def emit(self, final_sems):
        nc = self.nc
        with nc.Block() as block:
            def run(eng, e):
                for waits, fn, sem, inc in self.q[eng]:
                    for ws, v in waits:
                        e.wait_ge(ws.h, v)
                    ins = fn(e)
                    ins.then_inc(sem.h, inc)

            @block.tensor
            def _(t):
                run('pe', t)

            @block.scalar
            def _(a):
                run('act', a)

            @block.vector
            def _(v):
                run('dve', v)

            @block.gpsimd
            def _(g):
                run('pool', g)

            @block.sync
            def _(sp):
                run('sp', sp)
                for s in final_sems:
                    if s.count:
                        sp.wait_ge(s.h, s.count)


KB = 1024


class Ctx:
    pass


def setup(nc, stack, cfg):
    arena_bytes = cfg.get('arena_bytes', 206 * KB)
    arena = stack.enter_context(nc.sbuf_tensor("arena", [128, arena_bytes // 4], F32))
    psum = [stack.enter_context(nc.psum_tensor(f"ps{i}", [128, 512], F32)) for i in range(8)]
    B = Builder(nc, arena, arena_bytes, psum, stack)
    C = Ctx()
    C.cfg = cfg
    D, T = cfg['D'], cfg['T']
    C.NC = D // 128
    NC = C.NC
    TB = T * 4
    C.X0, C.H0, C.Y0, C.A0 = 0, 32 * KB, 48 * KB, 80 * KB
    C.W0 = 124 * KB
    C.M0 = 188 * KB
    C.xT = [Unit(B, C.X0 + c * 2 * KB, TB) for c in range(NC)]
    C.hT = [Unit(B, C.H0 + c * 1 * KB, TB // 2, BF16) for c in range(NC)]
    C.yT = [Unit(B, C.Y0 + c * 2 * KB, TB) for c in range(NC)]
    C.NFC = cfg['DFF'] // 128
    C.actT = [Unit(B, C.A0 + f * 1 * KB, TB // 2, BF16) for f in range(C.NFC)]
    C.sq = [Unit(B, C.A0 + c * 1 * KB, TB // 2, BF16) for c in range(NC)]
    C.stage = [Unit(B, C.A0 + b * 8 * KB, D * 4) for b in range(4)]
    B.wslots = [(Unit(B, C.W0 + s * 16 * KB, 16 * KB, BF16), B.new_sem(f"w{s}")) for s in range(4)]
    m = C.M0

    def alloc(nbytes, dtype=F32):
        nonlocal m
        u = Unit(B, m, nbytes, dtype)
        m += (nbytes + CELL - 1) // CELL * CELL
        return u
    C.alloc = alloc
    C.mptr = lambda: m
    C.ones_bf = alloc(256, BF16)
    C.ident = alloc(512)
    C.rstd = alloc(2 * KB)
    C.tmp = [Unit(B, C.Y0 + i * 2 * KB, 2 * KB) for i in range(2)]
    C.prm = [alloc(1 * KB) for _ in range(cfg['L'])]
    C.pstage = alloc(512)
    C.eps_u = C.ones_bf_pad = None
    C.sem_misc = B.new_sem("misc")
    C.sem_x = B.new_sem("xin")
    C.sem_out = [B.new_sem(f"out{i}") for i in range(4)]
    return B, C


def load_consts(B, C, d):
    B.dma('sp', C.ident.ap, d['c_ident'], [], [C.ident], C.sem_misc)
    B.op('dve', lambda v: v.memset(C.ones_bf.ap, 1.0), [], [C.ones_bf])
    C.eps_u = C.ones_bf
    e32 = B.arena[:, (C.ones_bf.off + 256) // 4:(C.ones_bf.off + 256) // 4 + 2]
    C.epsk = {1.0: e32[:, 0:1], 4.0: e32[:, 1:2]}
    B.op('dve', lambda v: v.memset(e32[:, 0:1], 1e-6), [], [C.ones_bf])
    B.op('dve', lambda v: v.memset(e32[:, 1:2], 4e-6), [], [C.ones_bf])


def load_params(B, C, d, l):
    NC = C.NC
    nj = 6
    src = d['norm_w'][l].rearrange("j (c p) -> (j c) p", p=128)
    st = C.pstage
    B.dma('sp', st.ap[0:nj * NC, 0:128], src, [], [st], C.sem_misc)
    ps, pk = B.ps()
    B.transpose(ps[:, 0:nj * NC], st.ap[0:nj * NC, 0:128], C.ident.ap[0:nj * NC, 0:nj * NC], [st, C.ident], [pk])
    B.op('dve', lambda v: v.tensor_copy(C.prm[l].ap[:, 0:nj * NC], ps[:, 0:nj * NC]), [pk], [C.prm[l]])
    if 'ssd_conv_w' in d:
        B.dma('sp', st.ap[0:32, 0:128], d['ssd_conv_w'][l].rearrange("k (c p) -> (k c) p", p=128), [], [st], C.sem_misc)
        B.dma('sp', st.ap[32:40, 0:128], d['ssd_conv_b'][l].rearrange("(c p) -> c p", p=128), [], [st], C.sem_misc)
        B.dma('sp', st.ap[40:44, 0:128], d['ssd_norm_w'][l].rearrange("(c p) -> c p", p=128), [], [st], C.sem_misc)
        B.dma('sp', st.ap[44:48, 0:128], d['ssd_d_rep'][l].rearrange("(c p) -> c p", p=128), [], [st], C.sem_misc)
        ps2, pk2 = B.ps()
        B.transpose(ps2[:, 0:48], st.ap[0:48, 0:128], C.ident.ap[0:48, 0:48], [st, C.ident], [pk2])
        B.op('dve', lambda v: v.tensor_copy(C.prm[l].ap[:, 96:144], ps2[:, 0:48]), [pk2], [C.prm[l]])


def nw_col(C, l, j, c):
    return C.prm[l].ap[:, j * C.NC + c: j * C.NC + c + 1]


def load_x_tile(B, C, x_rows, T, Treal=None):
    NC = C.NC
    Treal = T if Treal is None else Treal
    nb = (T + 127) // 128
    for b in range(nb):
        n = min(128, T - b * 128)
        r = max(0, min(128, Treal - b * 128))
        if r < n:
            B.op('dve', lambda v, b=b: v.memset(C.stage[b].ap, 0.0), [], [C.stage[b]])
        if r:
            B.dma('sp', C.stage[b].ap[0:r, :], x_rows[b * 128:b * 128 + r, :], [], [C.stage[b]], C.sem_x)
    for c in range(NC):
        ps, pk = B.ps()
        for b in range(nb):
            n = min(128, T - b * 128)
            B.transpose(ps[:, b * 128:b * 128 + n], C.stage[b].ap[0:n, c * 128:(c + 1) * 128],
                        C.ident.ap[0:n, 0:n], [C.stage[b], C.ident], [pk])
        if c % 2 == 0:
            B.op('act', lambda a, c=c, ps=ps: a.copy(C.xT[c].ap[:, 0:T], ps[:, 0:T]), [pk], [C.xT[c]])
        else:
            B.op('dve', lambda v, c=c, ps=ps: v.tensor_copy(C.xT[c].ap[:, 0:T], ps[:, 0:T]), [pk], [C.xT[c]])


def store_x_tile(B, C, y_rows, T):
    NC = C.NC
    nb = (T + 127) // 128
    for b in range(nb):
        n = min(128, T - b * 128)
        st = C.stage[b]
        for g in range(NC // 4):
            ps, pk = B.ps()
            for j in range(4):
                c = g * 4 + j
                B.transpose(ps[0:n, j * 128:(j + 1) * 128], C.xT[c].ap[:, b * 128:b * 128 + n],
                            C.ident.ap, [C.xT[c], C.ident], [pk])
            if g % 2 == 0:
                B.op('act', lambda a, g=g, ps=ps, st=st, n=n: a.copy(st.ap[0:n, g * 512:(g + 1) * 512], ps[0:n, :]), [pk], [st])
            else:
                B.op('dve', lambda v, g=g, ps=ps, st=st, n=n: v.tensor_copy(st.ap[0:n, g * 512:(g + 1) * 512], ps[0:n, :]), [pk], [st])
        B.dma('sp', y_rows[b * 128:b * 128 + n, :], st.ap[0:n, :], [st], [], C.sem_out[b])


def rms_rstd(B, C, src, T, nch, dim, sq_from_src=True, k=1.0):
    if sq_from_src:
        for c in range(nch):
            B.op('act', lambda a, c=c: a.activation(out=C.sq[c].ap[:, 0:T], in_=src[c].ap[:, 0:T], func=AF.Square),
                 [src[c]], [C.sq[c]])
    ps, pk = B.ps()
    B.mm(ps[:, 0:T], [(C.ones_bf.ap, C.sq[c].ap[:, 0:T]) for c in range(nch)], [C.ones_bf] + C.sq[:nch], [pk])
    B.op('act', lambda a: a.activation(out=C.rstd.ap[:, 0:T], in_=ps[:, 0:T], func=AF.Ln, bias=C.epsk[k], scale=k / dim), [pk, C.eps_u], [C.rstd])
    B.op('act', lambda a: a.activation(out=C.rstd.ap[:, 0:T], in_=C.rstd.ap[:, 0:T], func=AF.Exp, scale=-0.5), [C.rstd], [C.rstd])


def norm_to_h(B, C, l, j, T):
    rms_rstd(B, C, C.xT, T, C.NC, C.cfg['D'])
    B.dump(f"rstd_{l}_{j}", C.rstd.ap[:, 0:T], [C.rstd])
    B.dump(f"prm_{l}_{j}", C.prm[l].ap[:, 0:96], [C.prm[l]])
    for c in range(C.NC):
        B.op('dve', lambda v, c=c: v.scalar_tensor_tensor(C.hT[c].ap[:, 0:T], C.xT[c].ap[:, 0:T], nw_col(C, l, j, c),
                                                          C.rstd.ap[:, 0:T], ALU.mult, ALU.mult),
             [C.xT[c], C.rstd, C.prm[l]], [C.hT[c]])


RESID_ENG = 'dve'


def resid_add(B, C, l, j, T, half):
    rms_rstd(B, C, C.yT, T, C.NC, C.cfg['D'], sq_from_src=False, k=(4.0 if half else 1.0))
    for c in range(C.NC):
        B.op('dve', lambda v, c=c: v.scalar_tensor_tensor(C.yT[c].ap[:, 0:T], C.yT[c].ap[:, 0:T], nw_col(C, l, j, c),
                                                          C.rstd.ap[:, 0:T], ALU.mult, ALU.mult),
             [C.yT[c], C.rstd, C.prm[l]], [C.yT[c]])
        B.op(RESID_ENG, lambda g, c=c: g.tensor_tensor(C.xT[c].ap[:, 0:T], C.xT[c].ap[:, 0:T], C.yT[c].ap[:, 0:T], ALU.add),
             [C.xT[c], C.yT[c]], [C.xT[c]])


def ffn(B, C, d, l, i, T):
    NC, NFC = C.NC, C.NFC
    DFF = C.cfg['DFF']
    cut = C.cfg.get('cut', 99)
    if cut < 1:
        return
    norm_to_h(B, C, l, 4 * i, T)
    B.dump(f"h0_{l}_{i}", C.hT[0].ap[:, 0:T], [C.hT[0]])
    B.dump(f"h5_{l}_{i}", C.hT[5].ap[:, 0:T], [C.hT[5]])
    if cut < 2:
        return
    wg = d['w_ffn_gate'][l, i].rearrange("(c p) n -> p c n", p=128)
    wu = d['w_ffn_up'][l, i].rearrange("(c p) n -> p c n", p=128)
    wd = d['w_ffn_down'][l, i].rearrange("(f p) n -> p f n", p=128)
    ngrp = (NFC + 3) // 4
    for g in range(ngrp):
        nf = min(4, NFC - g * 4)
        def loader_g(slot, sem, g=g, nf=nf):
            o = slot.ap[:, 0:NC * 512].rearrange("p (c n) -> p c n", n=512)
            B.dma('pool', o[:, :, 0:nf * 128], wg[:, :, g * 512:g * 512 + nf * 128], [], [slot], sem)
        def loader_u(slot, sem, g=g, nf=nf):
            o = slot.ap[:, 0:NC * 512].rearrange("p (c n) -> p c n", n=512)
            B.dma('pool', o[:, :, 0:nf * 128], wu[:, :, g * 512:g * 512 + nf * 128], [], [slot], sem)
        slot_g = B.weights(loader_g)
        slot_u = B.weights(loader_u)
        sg = slot_g.ap[:, 0:NC * 512].rearrange("p (c n) -> p c n", n=512)
        su = slot_u.ap[:, 0:NC * 512].rearrange("p (c n) -> p c n", n=512)
        if g == 0:
            B.dump(f"wg_{l}_{i}", slot_g.ap[:, 0:1024], [slot_g])
            B.dump(f"wu_{l}_{i}", slot_u.ap[:, 0:1024], [slot_u])
        for j in range(nf):
            f = g * 4 + j
            pa, ka = B.ps()
            pg, kg = B.ps()
            B.mm(pa[:, 0:T], [(sg[:, c, j * 128:(j + 1) * 128], C.hT[c].ap[:, 0:T]) for c in range(NC)], [slot_g] + C.hT, [ka])
            B.mm(pg[:, 0:T], [(su[:, c, j * 128:(j + 1) * 128], C.hT[c].ap[:, 0:T]) for c in range(NC)], [slot_u] + C.hT, [kg])
            tmp = C.tmp[f % 2]
            B.op('act', lambda a, pa=pa, tmp=tmp: a.activation(out=tmp.ap[:, 0:T], in_=pa[:, 0:T], func=AF.Silu), [ka], [tmp])
            if f == 1:
                B.dump(f"silu1_{l}_{i}", tmp.ap[:, 0:T], [tmp])
            B.op('dve', lambda v, pg=pg, tmp=tmp, f=f: v.tensor_tensor(C.actT[f].ap[:, 0:T], tmp.ap[:, 0:T], pg[:, 0:T], ALU.mult),
                 [kg, tmp], [C.actT[f]])
    B.dump(f"act1_{l}_{i}", C.actT[1].ap[:, 0:T], [C.actT[1]])
    if cut < 3:
        return
    for p4 in range(NC // 4):
        banks = [B.ps() for _ in range(4)]
        nsl = (NFC + 15) // 16
        for sidx in range(nsl):
            f0 = sidx * 16
            nf = min(16, NFC - f0)
            def loader_d(slot, sem, f0=f0, nf=nf, p4=p4):
                o = slot.ap[:, 0:16 * 512].rearrange("p (f n) -> p f n", n=512)
                B.dma('pool', o[:, 0:nf, :], wd[:, f0:f0 + nf, p4 * 512:(p4 + 1) * 512], [], [slot], sem)
            slot = B.weights(loader_d)
            sd = slot.ap[:, 0:16 * 512].rearrange("p (f n) -> p f n", n=512)
            for j in range(4):
                py, ky = banks[j]
                def fn(t, py=py, sd=sd, j=j, f0=f0, nf=nf, sidx=sidx, nsl=nsl):
                    ins = None
                    for ff in range(nf):
                        ins = t.matmul(py[:, 0:T], sd[:, ff, j * 128:(j + 1) * 128], C.actT[f0 + ff].ap[:, 0:T],
                                       start=(sidx == 0 and ff == 0), stop=(sidx == nsl - 1 and ff == nf - 1))
                    return ins
                if C.cfg.get('var', 0) != 2:
                    B.op('pe', fn, [slot] + C.actT[f0:f0 + nf], [ky])
        for j in range(4):
            c = p4 * 4 + j
            py, ky = banks[j]
            if C.cfg.get('var', 0) == 1:
                continue
            B.op('dve', lambda v, c=c, py=py: v.tensor_copy(C.yT[c].ap[:, 0:T], py[:, 0:T]), [ky], [C.yT[c]])
            B.op('act', lambda a, c=c: a.activation(out=C.hT[c].ap[:, 0:T], in_=C.yT[c].ap[:, 0:T], func=AF.Square), [C.yT[c]], [C.hT[c]])
    B.dump(f"y3_{l}_{i}", C.yT[3].ap[:, 0:T], [C.yT[3]])
    if cut < 4:
        return
    sq_save = C.sq
    C.sq = C.hT
    resid_add(B, C, l, 4 * i + 1, T, half=True)
    C.sq = sq_save


ATTN_SCALE = 128 ** -0.5
GELU_K = 1.5957691216057308


class Stream:
    def __init__(self, L):
        self.past = [[] for _ in range(L)]
        self.npast_c = [0] * L
        self.k_out = self.v_out = self.lf_out = self.gm_out = self.conv_out = self.state_out = None
        self.tok0 = 0
        self.last = True


def setup_mixer(B, C):
    A, Y = C.A0, C.Y0
    bf1 = lambda off: Unit(B, off, 1 * KB, BF16)
    C.mixedT = [bf1(A + 28 * KB + i * KB) for i in range(16)]
    C.qT = [bf1(A + i * KB) for i in range(8)]
    C.kT = [bf1(A + 8 * KB + i * KB) for i in range(8)]
    C.vbf = [Unit(B, A + 16 * KB + b * 2 * KB, 2 * KB, BF16) for b in range(4)]
    C.pT = [bf1(A + 24 * KB + i * KB) for i in range(3)]
    C.kstage = [Unit(B, Y + b * 4 * KB, 4 * KB) for b in range(4)]
    C.vstage = [Unit(B, Y + 16 * KB + b * 4 * KB, 4 * KB) for b in range(4)]
    NR = 6
    C.NR = NR
    C.kblk = [Unit(B, Y + i * 512, 512) for i in range(NR)]
    C.kTblk = [Unit(B, Y + 3 * KB + i * 512, 256, BF16) for i in range(NR)]
    C.vblk = [Unit(B, Y + 6 * KB + i * 512, 256, BF16) for i in range(NR)]
    C.rc = Unit(B, Y + 9 * KB, 2 * KB)
    C.accs = Unit(B, Y + 11 * KB, 2 * KB)
    C.cqbc = Unit(B, Y + 13 * KB, 1 * KB, BF16)


def setup_small(B, C, L):
    al = C.alloc
    C.U_f = al(512)
    C.U_bf = al(256, BF16)
    C.ones_f = al(512)
    C.negm = al(512)
    C.ST = [al(2 * KB) for _ in range(L)]
    C.c_all = [al(1 * KB) for _ in range(L)]
    C.negc = al(1 * KB)
    C.convst = [al(512) for _ in range(L)]
    C.small = [al(512) for _ in range(L)]
    C.lf = al(512)
    C.fdtu = al(512)
    C.wfd = [al(512, BF16) for _ in range(L)]
    C.identbf = al(256, BF16)
    C.cq = C.rstd


def load_consts2(B, C, d):
    B.dma('sp', C.U_f.ap, d['c_tri'], [], [C.U_f], C.sem_misc)
    B.op('dve', lambda v: v.tensor_copy(C.U_bf.ap, C.U_f.ap), [C.U_f], [C.U_bf])
    B.op('dve', lambda v: v.memset(C.ones_f.ap, 1.0), [], [C.ones_f])
    B.op('dve', lambda v: v.tensor_copy(C.identbf.ap, C.ident.ap), [C.ident], [C.identbf])
    B.op('dve', lambda v: v.tensor_scalar(C.negm.ap, C.U_f.ap, -1.0, 30000.0, ALU.add, ALU.mult), [C.U_f], [C.negm])


def load_layer_small(B, C, d, l):
    sm = C.small[l]
    B.dma('sp', sm.ap[:, 0:8], d['fox_fb'][l:l + 1, :].partition_broadcast(128), [], [sm], C.sem_misc)
    B.dma('sp', sm.ap[:, 8:16], d['ssd_dt_bias'][l:l + 1, :].partition_broadcast(128), [], [sm], C.sem_misc)
    B.dma('sp', sm.ap[:, 16:24], d['ssd_a_log'][l:l + 1, :].partition_broadcast(128), [], [sm], C.sem_misc)
    win = d['w_in'][l].rearrange("(c p) n -> p c n", p=128)
    w3 = C.wfd[l].ap.rearrange("p (c k) -> p c k", k=16)
    B.dma('pool', w3[:, :, 0:8], win[:, :, 3072:3080], [], [C.wfd[l]], C.sem_wfd)
    B.dma('pool', w3[:, :, 8:16], win[:, :, 5640:5648], [], [C.wfd[l]], C.sem_wfd)
    B.op('act', lambda a: a.activation(out=sm.ap[:, 16:24], in_=sm.ap[:, 16:24], func=AF.Exp), [sm], [sm])
    B.op('dve', lambda v: v.tensor_scalar(sm.ap[:, 16:24], sm.ap[:, 16:24], -1.0, None, ALU.mult), [sm], [sm])


def slot3(slot):
    return slot.ap.rearrange("p (c n) -> p c n", n=512)


def proj_fm(B, C, win, col0, ncols, T, evac):
    def loader(slot, sem):
        B.dma('pool', slot3(slot)[:, :, 0:ncols], win[:, :, col0:col0 + ncols], [], [slot], sem)
    slot = B.weights(loader)
    s3 = slot3(slot)
    for j in range(ncols // 128):
        ps, pk = B.ps()
        B.mm(ps[:, 0:T], [(s3[:, c, j * 128:(j + 1) * 128], C.hT[c].ap[:, 0:T]) for c in range(C.NC)], [slot] + C.hT, [pk])
        evac(j, ps, pk)


def proj_tm(B, C, win, cols, T, evac):
    tot = sum(n for _, n in cols)

    def loader(slot, sem):
        o = 0
        for c0, n in cols:
            B.dma('pool', slot3(slot)[:, :, o:o + n], win[:, :, c0:c0 + n], [], [slot], sem)
            o += n
    slot = B.weights(loader)
    s3 = slot3(slot)
    nb = (T + 127) // 128
    for b in range(nb):
        n = min(128, T - b * 128)
        ps, pk = B.ps()
        B.mm(ps[0:n, 0:tot], [(C.hT[c].ap[:, b * 128:b * 128 + n], s3[:, c, 0:tot]) for c in range(C.NC)], [slot] + C.hT, [pk])
        evac(b, n, ps, pk)


def cumsum_blocks(B, C, l, src_of_block, nblocks, sizes, kb0):
    sm = C.small[l]
    ca = C.c_all[l].ap[:, 0:17 * 8].rearrange("p (k h) -> p k h", h=8)
    for b in range(nblocks):
        n = sizes[b]
        src, res = src_of_block(b)
        ps, pk = B.ps()
        B.mm(ps[0:n, 0:8], [(C.U_f.ap[0:n, 0:n], src)], [C.U_f] + res, [pk])
        B.mm(ps[:, 8:16], [(C.ones_f.ap[0:n, :], src)], [C.ones_f] + res, [pk])
        B.op('dve', lambda v, ps=ps, n=n, b=b: v.tensor_tensor(ca[0:n, kb0 + b, :], ps[0:n, 0:8], sm.ap[0:n, 24:32], ALU.add),
             [pk, sm], [C.c_all[l]])
        B.op('dve', lambda v, ps=ps: v.tensor_tensor(sm.ap[:, 24:32], ps[:, 8:16], sm.ap[:, 24:32], ALU.add), [pk, sm], [sm])


def fox_phase(B, C, d, l, T, S, win, Treal):
    nb = (T + 127) // 128
    bn = [min(128, T - b * 128) for b in range(nb)]
    bo = [max(0, min(128, Treal - b * 128)) for b in range(nb)]
    sm = C.small[l]
    for half in range(2):
        def ev_q(j, ps, pk, half=half):
            B.op('act', lambda a: a.mul(C.qT[half * 4 + j].ap[:, 0:T], ps[:, 0:T], ATTN_SCALE), [pk], [C.qT[half * 4 + j]])
        proj_fm(B, C, win, half * 512, 512, T, ev_q)
    for half in range(2):
        def ev_k(j, ps, pk, half=half):
            B.op('dve', lambda v: v.tensor_copy(C.kT[half * 4 + j].ap[:, 0:T], ps[:, 0:T]), [pk], [C.kT[half * 4 + j]])
        proj_fm(B, C, win, 1024 + half * 512, 512, T, ev_k)
    for b in range(nb):
        n = bn[b]
        for half in range(2):
            ps, pk = B.ps()
            psb = ps.bitcast(BF16)
            for j in range(4):
                h = half * 4 + j
                B.transpose(psb[0:n, j * 128:(j + 1) * 128], C.kT[h].ap[:, b * 128:b * 128 + n], C.identbf.ap, [C.kT[h], C.identbf], [pk])
            B.op('act', lambda a, b=b, n=n, half=half, psb=psb: a.copy(C.kstage[b].ap[0:n, half * 512:(half + 1) * 512], psb[0:n, 0:512]),
                 [pk], [C.kstage[b]])
    for b in range(nb):
        n = bo[b]
        if n == 0:
            continue
        B.dma('sp', S.k_out[l][S.tok0 + b * 128:S.tok0 + b * 128 + n, :], C.kstage[b].ap[0:n, :], [C.kstage[b]],
              [('kout', id(S), l, S.tok0 // 128 + b)], C.sem_kv[b])
    for half in range(2):
        def ev_vt(b, n, ps, pk, half=half):
            B.op('act', lambda a: a.copy(C.vstage[b].ap[0:n, half * 512:(half + 1) * 512], ps[0:n, 0:512]), [pk], [C.vstage[b]])
            B.op('dve', lambda v: v.tensor_copy(C.vbf[b].ap[0:n, half * 512:(half + 1) * 512],
                                                C.vstage[b].ap[0:n, half * 512:(half + 1) * 512]), [C.vstage[b]], [C.vbf[b]])
        proj_tm(B, C, win, [(2048 + half * 512, 512)], T, ev_vt)
    for b in range(nb):
        n = bo[b]
        if n == 0:
            continue
        B.dma('sp', S.v_out[l][S.tok0 + b * 128:S.tok0 + b * 128 + n, :], C.vstage[b].ap[0:n, :], [C.vstage[b]],
              [('vout', id(S), l, S.tok0 // 128 + b)], C.sem_kv[4 + b])
    fd = C.fdtu.ap[:, 0:64].rearrange("p (b k) -> p b k", k=16)

    w3 = C.wfd[l].ap.rearrange("p (c k) -> p c k", k=16)
    for b in range(nb):
        n = bn[b]
        ps, pk = B.ps()
        B.mm(ps[0:n, 0:16], [(C.hT[c].ap[:, b * 128:b * 128 + n], w3[:, c, :]) for c in range(C.NC)], [C.wfd[l]] + C.hT, [pk])
        B.op('dve', lambda v, b=b, n=n, ps=ps: v.tensor_copy(fd[0:n, b, :], ps[0:n, 0:16]), [pk], [C.fdtu])
    lf = C.lf.ap[:, 0:32].rearrange("p (b h) -> p b h", h=8)
    dt = C.lf.ap[:, 32:64].rearrange("p (b h) -> p b h", h=8)
    dtA = C.lf.ap[:, 64:96].rearrange("p (b h) -> p b h", h=8)
    for b in range(nb):
        n = bn[b]
        B.op('dve', lambda v, b=b, n=n: v.tensor_tensor(lf[0:n, b, :], fd[0:n, b, 0:8], sm.ap[0:n, 0:8], ALU.add), [C.fdtu, sm], [C.lf])
        B.op('dve', lambda v, b=b, n=n: v.tensor_tensor(dt[0:n, b, :], fd[0:n, b, 8:16], sm.ap[0:n, 8:16], ALU.add), [C.fdtu, sm], [C.lf])
    nbh = nb * 8
    P = 128 if T >= 128 else T
    B.op('act', lambda a: a.activation(out=C.lf.ap[0:P, 0:nbh], in_=C.lf.ap[0:P, 0:nbh], func=AF.Sigmoid), [C.lf], [C.lf])
    B.op('act', lambda a: a.activation(out=C.lf.ap[0:P, 0:nbh], in_=C.lf.ap[0:P, 0:nbh], func=AF.Ln), [C.lf], [C.lf])
    B.op('act', lambda a: a.activation(out=C.lf.ap[0:P, 32:32 + nbh], in_=C.lf.ap[0:P, 32:32 + nbh], func=AF.Exp), [C.lf], [C.lf])
    B.op('dve', lambda v: v.tensor_scalar(C.lf.ap[0:P, 32:32 + nbh], C.lf.ap[0:P, 32:32 + nbh], 1.0, None, ALU.add), [C.lf], [C.lf])
    B.op('act', lambda a: a.activation(out=C.lf.ap[0:P, 32:32 + nbh], in_=C.lf.ap[0:P, 32:32 + nbh], func=AF.Ln), [C.lf], [C.lf])
    for b in range(nb):
        n = bn[b]
        if bo[b] < n:
            assert bo[b] == 32 and n == 128
            B.op('dve', lambda v, b=b: v.memset(dt[32:64, b, :], 0.0), [C.lf], [C.lf])
            B.op('dve', lambda v, b=b: v.memset(dt[64:128, b, :], 0.0), [C.lf], [C.lf])
        B.op('dve', lambda v, b=b, n=n: v.tensor_tensor(dtA[0:n, b, :], dt[0:n, b, :], sm.ap[0:n, 16:24], ALU.mult), [C.lf, sm], [C.lf])
        if bo[b]:
            B.dma('sp', S.lf_out[l][S.tok0 + b * 128:S.tok0 + b * 128 + bo[b], :], lf[0:bo[b], b, :], [C.lf], [], C.sem_lf)
    kb0 = S.npast_c[l]
    cref = sm.ap[:, 32:40]
    B.op('dve', lambda v: v.tensor_copy(cref, sm.ap[:, 24:32]), [sm], [sm])
    cumsum_blocks(B, C, l, lambda b: (lf[0:bn[b], b, :], [C.lf]), nb, bn, kb0)
    nk = kb0 + nb
    ca = C.c_all[l].ap[:, 0:17 * 8].rearrange("p (k h) -> p k h", h=8)
    ng3 = C.negc.ap[:, 0:17 * 8].rearrange("p (k h) -> p k h", h=8)
    B.op('dve', lambda v: v.tensor_tensor(ng3[:, 0:nk, :], cref.unsqueeze(1).to_broadcast([128, nk, 8]), ca[:, 0:nk, :], ALU.subtract),
         [C.c_all[l], sm], [C.negc])
    chi = C.fdtu.ap[:, 64:64 + 16].bitcast(BF16).rearrange("p (b h) -> p b h", h=8)
    B.op('dve', lambda v: v.tensor_scalar(chi[:, 0:nb, :], ng3[:, kb0:kb0 + nb, :], -1.0, None, ALU.mult), [C.negc], [C.fdtu])
    ps, pk = B.ps()
    psb = ps.bitcast(BF16)
    for b in range(nb):
        n = bn[b]
        B.transpose(psb[0:8, b * 128:b * 128 + n], chi[0:n, b, :], C.identbf.ap[0:n, 0:n], [C.fdtu, C.identbf], [pk])
    cqb = C.cq.ap[0:8, 0:256].bitcast(BF16)
    B.op('act', lambda a, psb=psb: a.copy(cqb[0:8, 0:T], psb[0:8, 0:T]), [pk], [C.cq])
    blocks = [('past', kap, vap, key, n) for (kap, vap, key, n) in S.past[l]] + [('own', b) for b in range(nb)]
    accs = B.ps_reserve(4)
    nblk = len(blocks)
    units = [(h, bi) for h in range(8) for bi in range(nblk)]
    LA = 2
    stash = {}

    def head_begin(h):
        ps, pk = B.ps()
        B.mm(ps[:, 0:T], [(C.identbf.ap[0:8, h:h + 1].to_broadcast([8, 128]), cqb[0:8, 0:T])], [C.identbf, C.cq], [pk])
        B.op('act', lambda a, ps=ps: a.copy(C.cqbc.ap[:, 0:T], ps[:, 0:T]), [pk], [C.cqbc])

    def stage_a(u):
        h, bi = units[u]
        if bi == 0:
            head_begin(h)
        blk = blocks[bi]
        if blk[0] == 'past':
            _, kap, vap, key, n = blk
            kst, kTb, vb = C.kblk[u % C.NR], C.kTblk[u % C.NR], C.vblk[u % C.NR]
            B.dma('sp', kst.ap[0:n, 0:128], kap[:, h * 128:(h + 1) * 128], [key[0]], [kst], C.sem_kb[u % C.NR])
            pst, pstk = B.ps()
            B.transpose(pst[:, 0:n], kst.ap[0:n, 0:128], C.ident.ap[0:n, 0:n], [kst, C.ident], [pstk])
            B.op('act', lambda a, kTb=kTb, pst=pst, n=n: a.copy(kTb.ap[:, 0:n], pst[:, 0:n]), [pstk], [kTb])
            B.dma('pool', vb.ap[0:n, 0:128], vap[:, h * 128:(h + 1) * 128], [key[1]], [vb], C.sem_vb[u % C.NR])
            lhsK, lhsV, q0 = kTb.ap[:, 0:n], vb.ap[0:n, 0:128], 0
            rK, rV = [kTb], [vb]
            ncol = bi * 8 + h
            own = False
        else:
            b = blk[1]
            n = bn[b]
            lhsK, lhsV, q0 = C.kT[h].ap[:, b * 128:b * 128 + n], C.vbf[b].ap[0:n, h * 128:(h + 1) * 128], b * 128
            rK, rV = [C.kT[h]], [C.vbf[b]]
            ncol = (kb0 + b) * 8 + h
            own = True
        pss, pssk = B.ps()
        B.mm(pss[0:n, q0:T], [(lhsK, C.qT[h].ap[:, q0:T])], rK + [C.qT[h]], [pssk])
        pT = C.pT[u % 3]
        B.op('dve', lambda v, pss=pss, n=n, q0=q0: v.tensor_tensor(pss[0:n, q0:T], pss[0:n, q0:T], C.cqbc.ap[0:n, q0:T], ALU.add),
             [pssk, C.cqbc], [pssk])
        if own:
            B.op('dve', lambda v, pss=pss, n=n, q0=q0: v.tensor_tensor(pss[0:n, q0:q0 + n], pss[0:n, q0:q0 + n],
                                                                       C.negm.ap[0:n, 0:n], ALU.add), [pssk, C.negm], [pssk])
        B.op('act', lambda a, pT=pT, pss=pss, n=n, q0=q0, ncol=ncol: a.activation(
            out=pT.ap[0:n, q0:T], in_=pss[0:n, q0:T], func=AF.Exp, bias=C.negc.ap[0:n, ncol:ncol + 1], scale=1.0),
            [pssk, C.negc], [pT])
        stash[u] = (lhsV, rV, pT, n, q0)

    def stage_b(u):
        h, bi = units[u]
        lhsV, rV, pT, n, q0 = stash.pop(u)
        (po, pok), (pm, pmk) = accs[(h % 2) * 2], accs[(h % 2) * 2 + 1]
        B.op('pe', lambda t, po=po, lhsV=lhsV, pT=pT, n=n, q0=q0, first=(bi == 0), last=(bi == nblk - 1):
             t.matmul(po[:, q0:T], lhsV, pT.ap[0:n, q0:T], start=first, stop=last), rV + [pT], [pok])
        if bi == 0:
            B.op('dve', lambda v, pT=pT: v.tensor_copy(C.accs.ap[:, 0:T], pT.ap[:, 0:T]), [pT], [C.accs])
        else:
            B.op('dve', lambda v, pT=pT, n=n, q0=q0: v.tensor_tensor(C.accs.ap[0:n, q0:T], C.accs.ap[0:n, q0:T], pT.ap[0:n, q0:T], ALU.add),
                 [pT, C.accs], [C.accs])
        if bi == nblk - 1:
            B.mm(pm[:, 0:T], [(C.ones_f.ap, C.accs.ap[:, 0:T])], [C.ones_f, C.accs], [pmk])
            B.op('act', lambda a, pm=pm: a.activation(out=C.rc.ap[:, 0:T], in_=pm[:, 0:T], func=AF.Ln), [pmk], [C.rc])
            B.op('act', lambda a: a.activation(out=C.rc.ap[:, 0:T], in_=C.rc.ap[:, 0:T], func=AF.Exp, scale=-1.0), [C.rc], [C.rc])
            B.op('dve', lambda v, po=po, h=h: v.tensor_tensor(C.mixedT[h].ap[:, 0:T], po[:, 0:T], C.rc.ap[:, 0:T], ALU.mult),
                 [pok, C.rc], [C.mixedT[h]])
    for u in range(len(units) + LA):
        if u < len(units):
            stage_a(u)
        if u - LA >= 0:
            stage_b(u - LA)
    B.ps_release()


def setup_mixer2(B, C):
    A, Y = C.A0, C.Y0
    bf1 = lambda off: Unit(B, off, 1 * KB, BF16)
    C.uT = [bf1(A + i * KB) for i in range(4)]
    C.gtmp = [Unit(B, A + 4 * KB + i * 2 * KB, 2 * KB) for i in range(2)]
    C.gbuf = [Unit(B, Y + i * 2 * KB, 2 * KB) for i in range(2)]
    C.vbfg = [bf1(A + 8 * KB + b * KB) for b in range(4)]
    C.lnw = Unit(B, A + 12 * KB, 2 * KB)
    C.lnb = Unit(B, A + 14 * KB, 2 * KB)
    C.bsb = Unit(B, A + 16 * KB, 2 * KB)
    C.wmT = bf1(A + 18 * KB)
    C.wst = [Unit(B, A + 19 * KB + g * 512, 512) for g in range(4)]
    C.gst = Unit(B, A + 21 * KB, 512)
    C.zs = [bf1(A + i * KB) for i in range(4)]
    C.xpad = Unit(B, A + 4 * KB, 20 * KB)
    C.STbf = bf1(A + 24 * KB)
    C.cvstage = Unit(B, A + 26 * KB, 2 * KB)
    C.xact = [bf1(Y + i * KB) for i in range(8)]
    C.ysT = [Unit(B, Y + 8 * KB + i * 2 * KB, 2 * KB) for i in range(4)]
    C.Rb = Unit(B, Y + 16 * KB, 4 * KB)
    C.ctmp = [Unit(B, Y + 16 * KB + i * 2 * KB, 2 * KB) for i in range(2)]
    C.Lb = Unit(B, Y + 20 * KB, 4 * KB)
    C.Wb = Unit(B, Y + 24 * KB, 2 * KB, BF16)
    C.Cs = Unit(B, Y + 26 * KB, 2 * KB, BF16)
    C.xtm = bf1(Y + 28 * KB)
    C.xw = bf1(Y + 29 * KB)
    C.Btm = Unit(B, Y + 30 * KB, 512, BF16)
    C.cbm = Unit(B, Y + 30 * KB + 512, 1 * KB)
    C.ssm = Unit(B, Y + 31 * KB + 512, 512)
    C.sq4 = [bf1(Y + i * KB) for i in range(4)]


def gelu_ps(B, ps_ap, pk, out_ap, out_res, tmp, tmp_ap):
    B.op('act', lambda a: a.activation(out=tmp_ap, in_=ps_ap, func=AF.Square), [pk], [tmp])
    B.op('dve', lambda v: v.tensor_scalar(tmp_ap, tmp_ap, 0.044715, 1.0, ALU.mult, ALU.add), [tmp], [tmp])
    B.op('dve', lambda v: v.tensor_tensor(tmp_ap, tmp_ap, ps_ap, ALU.mult), [tmp, pk], [tmp])
    B.op('act', lambda a: a.activation(out=tmp_ap, in_=tmp_ap, func=AF.Sigmoid, scale=GELU_K), [tmp], [tmp])
    B.op('dve', lambda v: v.tensor_tensor(out_ap, tmp_ap, ps_ap, ALU.mult), [tmp, pk], out_res)


def gm_phase(B, C, d, l, T, S, win, Treal):
    nb = (T + 127) // 128
    bn = [min(128, T - b * 128) for b in range(nb)]
    bo = [max(0, min(128, Treal - b * 128)) for b in range(nb)]
    B.dma('sp', C.lnw.ap[:, 0:512], d['gm_ln_w'][l:l + 1, :].partition_broadcast(128), [], [C.lnw], C.sem_misc)
    B.dma('sp', C.lnb.ap[:, 0:512], d['gm_ln_b'][l:l + 1, :].partition_broadcast(128), [], [C.lnb], C.sem_misc)
    B.dma('sp', C.bsb.ap[:, 0:512], d['gm_bs'][l:l + 1].rearrange("o g t -> o (g t)").partition_broadcast(128), [], [C.bsb], C.sem_misc)
    ps, pk = B.ps()
    for g in range(4):
        B.dma('sp', C.wst[g].ap[:, 0:128], d['gm_ws'][l, g], [], [C.wst[g]], C.sem_misc)
        B.transpose(ps[:, g * 128:(g + 1) * 128], C.wst[g].ap[:, 0:128], C.ident.ap, [C.wst[g], C.ident], [pk])
    wm3 = C.wmT.ap.rearrange("p (g t) -> p g t", t=128)
    B.op('dve', lambda v, ps=ps: v.tensor_tensor(wm3, ps[:, 0:512].rearrange("p (g t) -> p g t", t=128),
                                          C.U_f.ap.unsqueeze(1).to_broadcast([128, 4, 128]), ALU.mult), [pk, C.U_f], [C.wmT])

    def ev_u(j, ps, pk):
        gelu_ps(B, ps[:, 0:T], pk, C.uT[j].ap[:, 0:T], [C.uT[j]], C.gtmp[j % 2], C.gtmp[j % 2].ap[:, 0:T])
    proj_fm(B, C, win, 3080, 512, T, ev_u)

    def ev_v(b, n, ps, pk):
        tmp, g = C.gtmp[b % 2], C.gbuf[b % 2]
        ga = g.ap[0:n, 0:512]
        st = C.gst.ap
        gelu_ps(B, ps[0:n, 0:512], pk, ga, [g], tmp, tmp.ap[0:n, 0:512])
        B.op('dve', lambda v: v.reduce_sum(st[0:n, 0:1], ga, AX.X), [g], [C.gst])
        B.op('dve', lambda v: v.tensor_scalar(st[0:n, 0:1], st[0:n, 0:1], -1.0 / 512, None, ALU.mult), [C.gst], [C.gst])
        B.op('dve', lambda v: v.tensor_scalar(ga, ga, st[0:n, 0:1], None, ALU.add), [g, C.gst], [g])
        B.op('act', lambda a: a.activation(out=tmp.ap[0:n, 0:512], in_=ga, func=AF.Square, accum_out=st[0:n, 1:2]), [g], [tmp, C.gst])
        B.op('dve', lambda v: v.tensor_scalar(st[0:n, 1:2], st[0:n, 1:2], 1.0 / 512, 1e-6, ALU.mult, ALU.add), [C.gst], [C.gst])
        B.op('act', lambda a: a.activation(out=st[0:n, 1:2], in_=st[0:n, 1:2], func=AF.Sqrt), [C.gst], [C.gst])
        B.op('dve', lambda v: v.reciprocal(st[0:n, 1:2], st[0:n, 1:2]), [C.gst], [C.gst])
        B.op('dve', lambda v: v.scalar_tensor_tensor(ga, ga, st[0:n, 1:2], C.lnw.ap[0:n, 0:512], ALU.mult, ALU.mult), [g, C.gst, C.lnw], [g])
        B.op('dve', lambda v: v.tensor_tensor(ga, ga, C.lnb.ap[0:n, 0:512], ALU.add), [g, C.lnb], [g])
        if S.gm_out is not None and bo[b]:
            B.dma('sp', S.gm_out[l][S.tok0 + b * 128:S.tok0 + b * 128 + bo[b], :], g.ap[0:bo[b], 0:512], [g], [], C.sem_gm[b % 2])
        B.op('act', lambda a: a.copy(C.vbfg[b].ap[0:n, 0:512], ga), [g], [C.vbfg[b]])
    proj_tm(B, C, win, [(3592, 512)], T, ev_v)
    bs3 = C.bsb.ap[:, 0:512].rearrange("p (g t) -> p g t", t=128)
    for g in range(4):
        ps, pk = B.ps()
        for b in range(nb):
            n = bn[b]
            B.mm(ps[:, b * 128:b * 128 + n], [(C.vbfg[b].ap[0:n, g * 128:(g + 1) * 128], wm3[0:n, g, 0:n])], [C.vbfg[b], C.wmT], [pk])
        tmp = C.gtmp[g % 2]
        if T >= 128:
            o3 = tmp.ap[:, 0:T].rearrange("p (b t) -> p b t", t=128)
            i3 = ps[:, 0:T].rearrange("p (b t) -> p b t", t=128)
            bb = bs3[:, g, :].unsqueeze(1).to_broadcast([128, nb, 128])
        else:
            o3, i3, bb = tmp.ap[:, 0:T], ps[:, 0:T], bs3[:, g, 0:T]
        B.op('dve', lambda v, o3=o3, i3=i3, bb=bb: v.tensor_tensor(o3, i3, bb, ALU.add), [pk, C.bsb], [tmp])
        B.op('dve', lambda v, g=g, tmp=tmp: v.tensor_tensor(C.mixedT[8 + g].ap[:, 0:T], tmp.ap[:, 0:T], C.uT[g].ap[:, 0:T], ALU.mult),
             [tmp, C.uT[g]], [C.mixedT[8 + g]])


def ssd_phase(B, C, d, l, T, S, win, Treal):
    nb = (T + 127) // 128
    bn = [min(128, T - b * 128) for b in range(nb)]
    prm = C.prm[l].ap
    cw = lambda k, c: prm[:, 96 + k * 8 + c:96 + k * 8 + c + 1]
    cb = lambda c: prm[:, 128 + c:129 + c]
    nwc = lambda ci: prm[:, 136 + ci:137 + ci]
    dcol = lambda ci: prm[:, 140 + ci:141 + ci]
    xp3 = C.xpad.ap.rearrange("p (c n) -> p c n", n=640)
    cs3 = C.convst[l].ap[:, 0:24].rearrange("p (c k) -> p c k", k=3)
    dt = C.lf.ap[:, 32:64].rearrange("p (b h) -> p b h", h=8)
    dtA = C.lf.ap[:, 64:96].rearrange("p (b h) -> p b h", h=8)
    def ev_z(j, ps, pk):
        B.op('act', lambda a: a.activation(out=C.zs[j].ap[:, 0:T], in_=ps[:, 0:T], func=AF.Silu), [pk], [C.zs[j]])
    proj_fm(B, C, win, 4104, 512, T, ev_z)
    B.op('dve', lambda v: v.tensor_copy(xp3[:, :, 0:3], cs3), [C.convst[l]], [C.xpad])
    for half in range(2):
        def ev_x(j, ps, pk, half=half):
            c = half * 4 + j
            if j % 2 == 0:
                B.op('act', lambda a: a.copy(xp3[:, c, 3:3 + T], ps[:, 0:T]), [pk], [C.xpad])
            else:
                B.op('dve', lambda v: v.tensor_copy(xp3[:, c, 3:3 + T], ps[:, 0:T]), [pk], [C.xpad])
        proj_fm(B, C, win, 4616 + half * 512, 512, T, ev_x)
    for c in range(8):
        tmp = C.ctmp[c % 2]
        ta = tmp.ap[:, 0:T]
        B.op('dve', lambda v, c=c, ta=ta: v.tensor_scalar(ta, xp3[:, c, 0:T], cw(0, c), cb(c), ALU.mult, ALU.add), [C.xpad, C.prm[l]], [tmp])
        for k in range(1, 4):
            B.op('dve', lambda v, c=c, ta=ta, k=k: v.scalar_tensor_tensor(ta, xp3[:, c, k:k + T], cw(k, c), ta, ALU.mult, ALU.add),
                 [C.xpad, C.prm[l], tmp], [tmp])
        B.op('act', lambda a, c=c, ta=ta: a.activation(out=C.xact[c].ap[:, 0:T], in_=ta, func=AF.Silu), [tmp], [C.xact[c]])
    B.op('dve', lambda v: v.tensor_copy(cs3, xp3[:, :, Treal:Treal + 3]), [C.xpad], [C.convst[l]])
    if S.last:
        ps, pk = B.ps()
        ps2, pk2 = B.ps()
        for c in range(8):
            pp, ppk = (ps, pk) if c < 4 else (ps2, pk2)
            B.transpose(pp[0:3, (c % 4) * 128:(c % 4 + 1) * 128], cs3[:, c, :], C.ident.ap, [C.convst[l], C.ident], [ppk])
        B.op('dve', lambda v, ps=ps: v.tensor_copy(C.cvstage.ap[0:3, 0:512], ps[0:3, 0:512]), [pk], [C.cvstage])
        B.dma('sp', S.conv_out[l][:, 0:512], C.cvstage.ap[0:3, 0:512], [C.cvstage], [], C.sem_cv)
        B.op('dve', lambda v, ps2=ps2: v.tensor_copy(C.cvstage.ap[0:3, 0:512], ps2[0:3, 0:512]), [pk2], [C.cvstage])
        B.dma('sp', S.conv_out[l][:, 512:1024], C.cvstage.ap[0:3, 0:512], [C.cvstage], [], C.sem_cv)
    B.op('act', lambda a: a.copy(C.STbf.ap, C.ST[l].ap), [C.ST[l]], [C.STbf])
    ST3 = C.ST[l].ap.rearrange("p (h q) -> p h q", q=64)
    R3 = C.Rb.ap.rearrange("p (h t) -> p h t", t=128)
    L3 = C.Lb.ap.rearrange("p (h t) -> p h t", t=128)
    W3 = C.Wb.ap.rearrange("p (h t) -> p h t", t=128)
    Cs3 = C.Cs.ap.rearrange("p (h t) -> p h t", t=128)
    cbm3 = C.cbm.ap.rearrange("p (g t) -> p g t", t=128)
    ssm = C.ssm.ap
    for b in range(nb):
        n = bn[b]
        t0 = b * 128
        dtA_b, dt_b = dtA[0:n, b, :], dt[0:n, b, :]
        ps, pk = B.ps()
        B.mm(ps[0:n, 0:8], [(C.U_f.ap[0:n, 0:n], dtA_b)], [C.U_f, C.lf], [pk])
        B.mm(ps[:, 8:16], [(C.ones_f.ap[0:n, :], dtA_b)], [C.ones_f, C.lf], [pk])
        B.op('dve', lambda v, ps=ps, n=n: v.tensor_copy(ssm[0:n, 0:8], ps[0:n, 0:8]), [pk], [C.ssm])
        B.op('dve', lambda v, ps=ps, n=n: v.tensor_tensor(ssm[0:n, 16:24], ps[0:n, 8:16], ssm[0:n, 0:8], ALU.subtract), [pk, C.ssm], [C.ssm])
        B.op('act', lambda a, ps=ps: a.activation(out=ssm[:, 24:32], in_=ps[:, 8:16], func=AF.Exp), [pk], [C.ssm])
        B.op('act', lambda a, n=n: a.activation(out=ssm[0:n, 16:24], in_=ssm[0:n, 16:24], func=AF.Exp), [C.ssm], [C.ssm])
        B.op('dve', lambda v, n=n, dt_b=dt_b: v.tensor_tensor(ssm[0:n, 16:24], ssm[0:n, 16:24], dt_b, ALU.mult), [C.ssm, C.lf], [C.ssm])
        B.op('dve', lambda v, n=n, dtA_b=dtA_b: v.tensor_tensor(R3[0:n, :, 0:n], dtA_b.unsqueeze(2).to_broadcast([n, 8, n]),
                                                                 C.U_f.ap[0:n, 0:n].unsqueeze(1).to_broadcast([n, 8, n]), ALU.mult),
             [C.lf, C.U_f], [C.Rb])
        pbs = []
        for X in range(2):
            pb, pbk = B.ps()
            pb3 = pb.rearrange("p (h t) -> p h t", t=128)
            if n == 128:
                B.mm(pb3[:, :, 0:n], [(C.ones_f.ap[0:n, :], R3[0:n, 4 * X:4 * X + 4, 0:n])], [C.ones_f, C.Rb], [pbk])
            else:
                for hh in range(4):
                    B.mm(pb3[:, hh, 0:n], [(C.ones_f.ap[0:n, :], R3[0:n, 4 * X + hh, 0:n])], [C.ones_f, C.Rb], [pbk])
            pbs.append((pb3, pbk))
        for X in range(2):
            pb3, pbk = pbs[X]
            B.op('act', lambda a, pb3=pb3, X=X, n=n: a.activation(out=L3[:, 4 * X:4 * X + 4, 0:n], in_=pb3[:, :, 0:n], func=AF.Exp), [pbk], [C.Lb])
        for g in range(2):
            B.op('dve', lambda v, g=g, n=n, t0=t0: v.tensor_tensor(Cs3[:, 4 * g:4 * g + 4, 0:n], L3[:, 4 * g:4 * g + 4, 0:n],
                                                                   C.xact[6 + g].ap[:, t0:t0 + n].unsqueeze(1).to_broadcast([128, 4, n]), ALU.mult),
                 [C.Lb, C.xact[6 + g]], [C.Cs])
        for X in range(2):
            pb3, pbk = pbs[X]
            B.op('dve', lambda v, pb3=pb3, X=X, n=n: v.tensor_tensor(L3[0:n, 4 * X:4 * X + 4, 0:n], pb3[0:n, :, 0:n],
                                                                     ssm[0:n, 4 * X:4 * X + 4].unsqueeze(2).to_broadcast([n, 4, n]), ALU.subtract),
                 [pbk, C.ssm, C.Lb], [C.Lb])
        B.op('dve', lambda v, n=n: v.tensor_scalar(L3[0:n, :, 0:n], L3[0:n, :, 0:n], 0.0, None, ALU.min), [C.Lb], [C.Lb])
        B.op('act', lambda a, n=n: a.activation(out=L3[0:n, :, 0:n], in_=L3[0:n, :, 0:n], func=AF.Exp), [C.Lb], [C.Lb])
        pcb, pcbk = B.ps()
        for g in range(2):
            B.mm(pcb[0:n, g * 128:g * 128 + n], [(C.xact[4 + g].ap[:, t0:t0 + n], C.xact[6 + g].ap[:, t0:t0 + n])],
                 [C.xact[4 + g], C.xact[6 + g]], [pcbk])
        pcb3 = pcb[:, 0:256].rearrange("p (g t) -> p g t", t=128)
        B.op('dve', lambda v, n=n, pcb3=pcb3: v.tensor_tensor(cbm3[0:n, :, 0:n], pcb3[0:n, :, 0:n],
                                                              C.U_f.ap[0:n, 0:n].unsqueeze(1).to_broadcast([n, 2, n]), ALU.mult), [pcbk, C.U_f], [C.cbm])
        for g in range(2):
            B.op('dve', lambda v, g=g, n=n: v.tensor_tensor(L3[0:n, 4 * g:4 * g + 4, 0:n], L3[0:n, 4 * g:4 * g + 4, 0:n],
                                                            cbm3[0:n, g, 0:n].unsqueeze(1).to_broadcast([n, 4, n]), ALU.mult), [C.Lb, C.cbm], [C.Lb])
        B.op('dve', lambda v, n=n, dt_b=dt_b: v.tensor_tensor(W3[0:n, :, 0:n], L3[0:n, :, 0:n], dt_b.unsqueeze(2).to_broadcast([n, 8, n]), ALU.mult),
             [C.Lb, C.lf], [C.Wb])
        ptx, ptxk = B.ps()
        ptxb = ptx.bitcast(BF16)
        for ci in range(4):
            B.transpose(ptxb[0:n, ci * 128:(ci + 1) * 128], C.xact[ci].ap[:, t0:t0 + n], C.identbf.ap, [C.xact[ci], C.identbf], [ptxk])
        for g in range(2):
            B.transpose(ptxb[0:n, 512 + g * 128:512 + (g + 1) * 128], C.xact[4 + g].ap[:, t0:t0 + n], C.identbf.ap, [C.xact[4 + g], C.identbf], [ptxk])
        B.op('act', lambda a, n=n, ptxb=ptxb: a.copy(C.xtm.ap[0:n, 0:512], ptxb[0:n, 0:512]), [ptxk], [C.xtm])
        B.op('act', lambda a, n=n, ptxb=ptxb: a.copy(C.Btm.ap[0:n, 0:256], ptxb[0:n, 512:768]), [ptxk], [C.Btm])
        B.op('dve', lambda v, n=n: v.tensor_tensor(C.xw.ap[0:n, 0:512].rearrange("p (h q) -> p h q", q=64),
                                                   C.xtm.ap[0:n, 0:512].rearrange("p (h q) -> p h q", q=64),
                                                   ssm[0:n, 16:24].unsqueeze(2).to_broadcast([n, 8, 64]), ALU.mult), [C.xtm, C.ssm], [C.xw])
        py, pyk = B.ps()

        def fn_y(t, n=n, py=py):
            ins = None
            for h in range(8):
                o = py[(h % 2) * 64:(h % 2) * 64 + 64, (h // 2) * 128:(h // 2) * 128 + n]
                t.matmul(o, C.xtm.ap[0:n, h * 64:(h + 1) * 64], W3[0:n, h, 0:n], start=True, stop=False)
                ins = t.matmul(o, C.STbf.ap[:, h * 64:(h + 1) * 64], Cs3[:, h, 0:n], start=False, stop=True)
            return ins
        B.op('pe', fn_y, [C.xtm, C.Wb, C.STbf, C.Cs], [pyk])
        for ci in range(4):
            B.op('dve', lambda v, ci=ci, n=n, t0=t0, py=py: v.scalar_tensor_tensor(
                C.ysT[ci].ap[:, t0:t0 + n], C.xact[ci].ap[:, t0:t0 + n], dcol(ci), py[:, ci * 128:ci * 128 + n], ALU.mult, ALU.add),
                [C.xact[ci], C.prm[l], pyk], [C.ysT[ci]])
        pst, pstk = B.ps()
        for g in range(2):
            B.mm(pst[:, g * 256:(g + 1) * 256], [(C.Btm.ap[0:n, g * 128:(g + 1) * 128], C.xw.ap[0:n, g * 256:(g + 1) * 256])], [C.Btm, C.xw], [pstk])
        B.op('dve', lambda v: v.tensor_tensor(ST3, ST3, ssm[:, 24:32].unsqueeze(2).to_broadcast([128, 8, 64]), ALU.mult), [C.ST[l], C.ssm], [C.ST[l]])
        B.op('dve', lambda v, pst=pst: v.tensor_tensor(C.ST[l].ap, C.ST[l].ap, pst[:, 0:512], ALU.add), [C.ST[l], pstk], [C.ST[l]])
        B.op('act', lambda a: a.copy(C.STbf.ap, C.ST[l].ap), [C.ST[l]], [C.STbf])
    for ci in range(4):
        B.op('dve', lambda v, ci=ci: v.tensor_tensor(C.ysT[ci].ap[:, 0:T], C.ysT[ci].ap[:, 0:T], C.zs[ci].ap[:, 0:T], ALU.mult),
             [C.ysT[ci], C.zs[ci]], [C.ysT[ci]])
        B.op('act', lambda a, ci=ci: a.activation(out=C.sq4[ci].ap[:, 0:T], in_=C.ysT[ci].ap[:, 0:T], func=AF.Square), [C.ysT[ci]], [C.sq4[ci]])
    sq_save = C.sq
    C.sq = C.sq4
    rms_rstd(B, C, None, T, 4, 512, sq_from_src=False)
    C.sq = sq_save
    for ci in range(4):
        B.op('dve', lambda v, ci=ci: v.scalar_tensor_tensor(C.mixedT[12 + ci].ap[:, 0:T], C.ysT[ci].ap[:, 0:T], nwc(ci), C.rstd.ap[:, 0:T],
                                                            ALU.mult, ALU.mult), [C.ysT[ci], C.prm[l], C.rstd], [C.mixedT[12 + ci]])
    if S.last:
        for ci in range(4):
            ps, pk = B.ps()
            B.transpose(ps[:, 0:128], C.ST[l].ap[:, ci * 128:(ci + 1) * 128], C.ident.ap, [C.ST[l], C.ident], [pk])
            B.op('dve', lambda v, ps=ps: v.tensor_copy(C.cvstage.ap[:, 0:128], ps[:, 0:128]), [pk], [C.cvstage])
            B.dma('sp', S.state_out[l][ci * 128:(ci + 1) * 128, :], C.cvstage.ap[:, 0:128], [C.cvstage], [], C.sem_cv)


def out_proj(B, C, d, l, T):
    wo = d['w_out'][l].rearrange("(c p) n -> p c n", p=128)
    for q4 in range(4):
        def loader(slot, sem, q4=q4):
            B.dma('pool', slot3(slot), wo[:, :, q4 * 512:(q4 + 1) * 512], [], [slot], sem)
        slot = B.weights(loader)
        s3 = slot3(slot)
        for j in range(4):
            c = q4 * 4 + j
            ps, pk = B.ps()
            B.mm(ps[:, 0:T], [(s3[:, n, j * 128:(j + 1) * 128], C.mixedT[n].ap[:, 0:T]) for n in range(16)], [slot] + C.mixedT, [pk])
            B.op('dve', lambda v, c=c, ps=ps: v.tensor_copy(C.yT[c].ap[:, 0:T], ps[:, 0:T]), [pk], [C.yT[c]])
            B.op('act', lambda a, c=c: a.activation(out=C.hT[c].ap[:, 0:T], in_=C.yT[c].ap[:, 0:T], func=AF.Square), [C.yT[c]], [C.hT[c]])
    sq_save = C.sq
    C.sq = C.hT
    resid_add(B, C, l, 3, T, half=False)
    C.sq = sq_save


def mixer(B, C, d, l, T, S, Treal=None):
    Treal = T if Treal is None else Treal
    nb = (T + 127) // 128
    norm_to_h(B, C, l, 2, T)
    win = d['w_in'][l].rearrange("(c p) n -> p c n", p=128)
    fox_phase(B, C, d, l, T, S, win, Treal)
    for i in range(8):
        B.dump(f"mx{i}", C.mixedT[i].ap[:, 0:T], [C.mixedT[i]])
    gm_phase(B, C, d, l, T, S, win, Treal)
    for i in range(8, 12):
        B.dump(f"mx{i}", C.mixedT[i].ap[:, 0:T], [C.mixedT[i]])
    ssd_phase(B, C, d, l, T, S, win, Treal)
    for i in range(12, 16):
        B.dump(f"mx{i}", C.mixedT[i].ap[:, 0:T], [C.mixedT[i]])
    out_proj(B, C, d, l, T)
    S.npast_c[l] += nb


def stream_begin(B, C, d, S, l, cache=None):
    sm = C.small[l]
    B.op('dve', lambda v: v.memset(sm.ap[:, 24:32], 0.0), [], [sm])
    B.op('dve', lambda v: v.memset(C.c_all[l].ap, 0.0), [], [C.c_all[l]])
    if cache is None:
        B.op('dve', lambda v: v.memset(C.ST[l].ap, 0.0), [], [C.ST[l]])
        B.op('dve', lambda v: v.memset(C.convst[l].ap, 0.0), [], [C.convst[l]])
        return
    P = cache['ck'].shape[0]
    npb = P // 128
    tmp = C.cvstage
    t3 = tmp.ap[:, 0:npb * 8].rearrange("p (k h) -> p k h", h=8)
    B.dma('sp', t3, cache['clf'].rearrange("(k p) h -> p k h", p=128), [], [tmp], C.sem_misc)
    cumsum_blocks(B, C, l, lambda b: (t3[:, b, :], [tmp]), npb, [128] * npb, 0)
    S.npast_c[l] = npb
    S.past[l] = [(cache['ck'][b * 128:(b + 1) * 128, :], cache['cv'][b * 128:(b + 1) * 128, :], (('ck', l, b), ('cv', l, b)), 128)
                 for b in range(npb)]
    cs3 = C.convst[l].ap[:, 0:24].rearrange("p (c k) -> p c k", k=3)
    st = C.kblk[0]
    for half in range(2):
        B.dma('sp', st.ap[0:3, 0:128], cache['conv'][:, 0:128], [], [st], C.sem_misc) if False else None
    cvs = C.gbuf[0]
    cvs2 = C.gbuf[1]
    B.dma('sp', cvs.ap[0:3, 0:512], cache['conv'][:, 0:512], [], [cvs], C.sem_misc)
    B.dma('sp', cvs2.ap[0:3, 0:512], cache['conv'][:, 512:1024], [], [cvs2], C.sem_misc)
    ps, pk = B.ps()
    for c in range(8):
        src = cvs if c < 4 else cvs2
        B.transpose(ps[:, c * 3:c * 3 + 3], src.ap[0:3, (c % 4) * 128:(c % 4 + 1) * 128], C.ident.ap[0:3, 0:3], [src, C.ident], [pk])
    B.op('dve', lambda v, ps=ps: v.tensor_copy(C.convst[l].ap[:, 0:24], ps[:, 0:24]), [pk], [C.convst[l]])
    ps, pk = B.ps()
    for ci in range(4):
        stg = C.wst[ci]
        B.dma('sp', stg.ap[:, 0:128], cache['state'][ci * 128:(ci + 1) * 128, :], [], [stg], C.sem_misc)
        B.transpose(ps[:, ci * 128:(ci + 1) * 128], stg.ap[:, 0:128], C.ident.ap, [stg, C.ident], [pk])
    B.op('dve', lambda v, ps=ps: v.tensor_copy(C.ST[l].ap, ps[:, 0:512]), [pk], [C.ST[l]])


L_, D_, DFF_, NIN_ = 2, 2048, 5632, 5648
SEQ_, PAST_, DECL_ = 2048, 1024, 32
TILE_ = 512


def build_full(n_tiles=SEQ_ // TILE_, do_sample=True):
    nc = bass.Bass("TRN2", target_bir_lowering=False)
    cfg = dict(D=D_, DFF=DFF_, T=TILE_, L=L_, arena_bytes=207 * 1024)
    d = {}

    def inp(name, shape):
        d[name] = nc.dram_tensor(name, shape, F32, kind="ExternalInput").ap()

    def outp(name, shape):
        d[name] = nc.dram_tensor(name, shape, F32, kind="ExternalOutput").ap()
    inp('x_p', [SEQ_, D_]); inp('x_s', [DECL_, D_])
    inp('ck', [L_, PAST_, 1024]); inp('cv', [L_, PAST_, 1024]); inp('clf', [L_, PAST_, 8])
    inp('cconv', [L_, 3, 1024]); inp('cstate', [L_, 512, 128])
    inp('norm_w', [L_, 6, D_]); inp('w_ffn_gate', [L_, 2, D_, DFF_]); inp('w_ffn_up', [L_, 2, D_, DFF_]); inp('w_ffn_down', [L_, 2, DFF_, D_])
    inp('w_in', [L_, D_, NIN_]); inp('w_out', [L_, D_, D_])
    inp('fox_fb', [L_, 8]); inp('gm_ln_w', [L_, 512]); inp('gm_ln_b', [L_, 512]); inp('gm_ws', [L_, 4, 128, 128]); inp('gm_bs', [L_, 4, 128])
    inp('ssd_conv_w', [L_, 4, 1024]); inp('ssd_conv_b', [L_, 1024]); inp('ssd_dt_bias', [L_, 8]); inp('ssd_a_log', [L_, 8])
    inp('ssd_d_rep', [L_, 512]); inp('ssd_norm_w', [L_, 512])
    inp('c_ident', [128, 128]); inp('c_tri', [128, 128])
    outp('y_p', [SEQ_, D_]); outp('p_k', [L_, SEQ_, 1024]); outp('p_v', [L_, SEQ_, 1024]); outp('p_lf', [L_, SEQ_, 8])
    outp('p_conv', [L_, 3, 1024]); outp('p_state', [L_, 512, 128])
    outp('y_s', [DECL_, D_]); outp('s_k', [L_, DECL_, 1024]); outp('s_v', [L_, DECL_, 1024]); outp('s_lf', [L_, DECL_, 8])
    outp('s_gm', [L_, DECL_, 512]); outp('s_conv', [L_, 3, 1024]); outp('s_state', [L_, 512, 128])
    with ExitStack() as stack:
        B, C = setup(nc, stack, cfg)
        setup_small(B, C, L_)
        setup_mixer(B, C)
        setup_mixer2(B, C)
        C.sem_kv = [B.new_sem(f"kv{i}") for i in range(8)]
        C.sem_lf = B.new_sem("lf")
        C.sem_kb = [B.new_sem(f"kb{i}") for i in range(6)]
        C.sem_vb = [B.new_sem(f"vb{i}") for i in range(6)]
        C.sem_wfd = B.new_sem("wfd")
        C.sem_gm = [B.new_sem(f"gm{i}") for i in range(2)]
        C.sem_cv = B.new_sem("cv")

        def program():
            load_consts(B, C, d)
            load_consts2(B, C, d)
            for l in range(L_):
                load_params(B, C, d, l)
                load_layer_small(B, C, d, l)
            Sp = Stream(L_)
            Sp.k_out = [d['p_k'][l] for l in range(L_)]
            Sp.v_out = [d['p_v'][l] for l in range(L_)]
            Sp.lf_out = [d['p_lf'][l] for l in range(L_)]
            Sp.conv_out = [d['p_conv'][l] for l in range(L_)]
            Sp.state_out = [d['p_state'][l] for l in range(L_)]
            for l in range(L_):
                stream_begin(B, C, d, Sp, l, None)
            for j in range(n_tiles):
                T = TILE_
                Sp.tok0 = j * T
                Sp.last = (j == n_tiles - 1)
                load_x_tile(B, C, d['x_p'][j * T:(j + 1) * T, :], T)
                for l in range(L_):
                    ffn(B, C, d, l, 0, T)
                    mixer(B, C, d, l, T, Sp)
                    ffn(B, C, d, l, 1, T)
                    for b in range(T // 128):
                        r0 = j * T + b * 128
                        kb = j * (T // 128) + b
                        Sp.past[l].append((d['p_k'][l][r0:r0 + 128, :], d['p_v'][l][r0:r0 + 128, :],
                                           (('kout', id(Sp), l, kb), ('vout', id(Sp), l, kb)), 128))
                store_x_tile(B, C, d['y_p'][j * T:(j + 1) * T, :], T)
            if do_sample:
                Ss = Stream(L_)
                Ss.k_out = [d['s_k'][l] for l in range(L_)]
                Ss.v_out = [d['s_v'][l] for l in range(L_)]
                Ss.lf_out = [d['s_lf'][l] for l in range(L_)]
                Ss.gm_out = [d['s_gm'][l] for l in range(L_)]
                Ss.conv_out = [d['s_conv'][l] for l in range(L_)]
                Ss.state_out = [d['s_state'][l] for l in range(L_)]
                for l in range(L_):
                    stream_begin(B, C, d, Ss, l, dict(ck=d['ck'][l], cv=d['cv'][l], clf=d['clf'][l], conv=d['cconv'][l], state=d['cstate'][l]))
                T, TR = 128, DECL_
                load_x_tile(B, C, d['x_s'], T, TR)
                for l in range(L_):
                    ffn(B, C, d, l, 0, T)
                    mixer(B, C, d, l, T, Ss, TR)
                    ffn(B, C, d, l, 1, T)
                store_x_tile(B, C, d['y_s'], TR)
        B.dry = True
        program()
        B.dry = False
        program()
        B.emit(B.dsems)
    return nc


_NC_CACHE = {}


def kernel(x_prompt, x_sample, cache_fox_k, cache_fox_v, cache_fox_logf, state_ssd_conv, state_ssd,
           norm_w, w_ffn_gate, w_ffn_up, w_ffn_down, w_in, fox_fb, gm_ln_w, gm_ln_b, gm_ws, gm_bs,
           ssd_conv_w, ssd_conv_b, ssd_dt_bias, ssd_a_log, ssd_d, ssd_norm_w, w_out):
    f32 = lambda a: np.ascontiguousarray(np.asarray(a, dtype=np.float32))
    n_cores = 8
    if 'nc' not in _NC_CACHE:
        _NC_CACHE['nc'] = build_full()
    nc = _NC_CACHE['nc']
    shared = dict(
        norm_w=f32(norm_w), w_ffn_gate=f32(w_ffn_gate), w_ffn_up=f32(w_ffn_up), w_ffn_down=f32(w_ffn_down),
        w_in=f32(w_in), w_out=f32(w_out), fox_fb=f32(fox_fb), gm_ln_w=f32(gm_ln_w), gm_ln_b=f32(gm_ln_b),
        gm_ws=f32(gm_ws), gm_bs=f32(gm_bs), ssd_conv_w=f32(ssd_conv_w), ssd_conv_b=f32(ssd_conv_b),
        ssd_dt_bias=f32(ssd_dt_bias), ssd_a_log=f32(ssd_a_log), ssd_norm_w=f32(ssd_norm_w),
        ssd_d_rep=f32(np.repeat(np.asarray(ssd_d, dtype=np.float32), 64, axis=1)),
        c_ident=np.eye(128, dtype=np.float32), c_tri=np.triu(np.ones((128, 128), np.float32)),
    )
    xp, xs = f32(x_prompt), f32(x_sample)
    ck, cv, clf = f32(cache_fox_k), f32(cache_fox_v), f32(cache_fox_logf)
    cconv, cst = f32(state_ssd_conv), f32(state_ssd)
    in_maps = []
    for c in range(n_cores):
        m = dict(shared)
        m['x_p'] = xp[c % 4]
        m['x_s'] = xs[c]
        m['ck'] = np.ascontiguousarray(ck[:, c].reshape(L_, PAST_, 1024))
        m['cv'] = np.ascontiguousarray(cv[:, c].reshape(L_, PAST_, 1024))
        m['clf'] = np.ascontiguousarray(clf[:, c])
        m['cconv'] = np.ascontiguousarray(cconv[:, c])
        m['cstate'] = np.ascontiguousarray(cst[:, c].reshape(L_, 512, 128))
        in_maps.append(m)
    res = run_bass_kernel_spmd(nc, in_maps, core_ids=list(range(n_cores)))
    r = res.results
    st = lambda key, cores: np.stack([r[c][key] for c in cores])
    P4, S8 = range(4), range(8)
    y_prompt = st('y_p', P4)
    y_sample = st('y_s', S8)
    p_k = np.transpose(st('p_k', P4), (1, 0, 2, 3)).reshape(L_, 4, SEQ_, 8, 128)
    p_v = np.transpose(st('p_v', P4), (1, 0, 2, 3)).reshape(L_, 4, SEQ_, 8, 128)
    p_lf = np.transpose(st('p_lf', P4), (1, 0, 2, 3))
    p_conv = np.transpose(st('p_conv', P4), (1, 0, 2, 3))
    p_ssm = np.transpose(st('p_state', P4), (1, 0, 2, 3)).reshape(L_, 4, 8, 64, 128)
    s_k = np.transpose(st('s_k', S8), (1, 0, 2, 3)).reshape(L_, 8, DECL_, 8, 128)
    s_v = np.transpose(st('s_v', S8), (1, 0, 2, 3)).reshape(L_, 8, DECL_, 8, 128)
    s_lf = np.transpose(st('s_lf', S8), (1, 0, 2, 3))
    s_gm = np.transpose(st('s_gm', S8), (1, 0, 2, 3))
    s_conv = np.transpose(st('s_conv', S8), (1, 0, 2, 3))
    s_ssm = np.transpose(st('s_state', S8), (1, 0, 2, 3)).reshape(L_, 8, 8, 64, 128)
    outs = (y_prompt, y_sample, p_k, p_v, p_lf, p_conv, p_ssm, s_k, s_v, s_lf, s_gm, s_conv, s_ssm)
    return tuple(np.ascontiguousarray(o, dtype=np.float32) for o in outs)
```

```python
from contextlib import ExitStack
import numpy as np
import concourse.bass as bass
import concourse.mybir as mybir
from concourse.bass_utils import run_bass_kernel_spmd

F32, BF16 = mybir.dt.float32, mybir.dt.bfloat16
AF = mybir.ActivationFunctionType
ALU = mybir.AluOpType
AX = mybir.AxisListType
CELL = 512
SELF_WAIT = {'pe': False, 'act': True, 'dve': True, 'pool': True, 'sp': False}


class Sem:
    def __init__(self, h, name, is_dma):
        self.h = h
        self.name = name
        self.is_dma = is_dma
        self.count = 0


class Unit:
    def __init__(self, B, off, nbytes, dtype=F32):
        assert off % CELL == 0 and nbytes % 4 == 0, (off, nbytes)
        assert off + nbytes <= B.arena_bytes, (off, nbytes, B.arena_bytes)
        self.off, self.nbytes = off, nbytes
        self.cells = [('sb', i) for i in range(off // CELL, (off + nbytes + CELL - 1) // CELL)]
        ap = B.arena[:, off // 4:(off + nbytes) // 4]
        if dtype == BF16:
            ap = ap.bitcast(BF16)
        self.ap = ap


class Builder:
    def __init__(self, nc, arena, arena_bytes, psum, stack):
        self.nc = nc
        self.arena = arena
        self.arena_bytes = arena_bytes
        self.psum = psum
        self.stack = stack
        self.q = {e: [] for e in ('pe', 'act', 'dve', 'pool', 'sp')}
        self.seen = {e: {} for e in self.q}
        self.cells = {}
        self.esem = {e: self.new_sem('e_' + e, False) for e in ('pe', 'act', 'dve', 'pool')}
        self.dsems = []
        self.ps_rr = 0
        self.dry = False
        self.wreq = []
        self.wi = 0
        self.wissued = 0
        self.wslots = None
        self.nops = {e: 0 for e in self.q}

    def new_sem(self, name, is_dma=True):
        h = self.stack.enter_context(self.nc.semaphore(name))
        s = Sem(h, name, is_dma)
        if is_dma:
            self.dsems.append(s)
        return s

    def _cells(self, res):
        out = []
        for r in res:
            if isinstance(r, Unit):
                out.extend(r.cells)
            elif isinstance(r, (list, tuple)) and r and isinstance(r[0], Unit):
                for u in r:
                    out.extend(u.cells)
            else:
                out.append(r)
        return out

    def _deps(self, eng, reads, writes):
        need = {}

        def add(ev):
            sem, v = ev
            if sem.is_dma:
                v = sem.count
            elif sem is self.esem.get(eng) and not SELF_WAIT[eng]:
                return
            if need.get(sem, 0) < v:
                need[sem] = v
        for c in reads:
            cell = self.cells.get(c)
            if cell and cell[0]:
                add(cell[0])
            if cell and isinstance(c, tuple) and c[0] == 'ps':
                for ev in cell[1].values():
                    if ev[0] is not self.esem.get(eng):
                        add(ev)
        for c in writes:
            cell = self.cells.get(c)
            if cell:
                if cell[0]:
                    add(cell[0])
                for ev in cell[1].values():
                    add(ev)
        waits = []
        seen = self.seen[eng]
        for sem, v in need.items():
            if seen.get(sem, 0) < v:
                seen[sem] = v
                waits.append((sem, v))
        return waits

    def _commit(self, ev, reads, writes):
        sem, v = ev
        for c in reads:
            cell = self.cells.setdefault(c, [None, {}])
            old = cell[1].get(sem)
            if old is None or old[1] < v:
                cell[1][sem] = ev
        for c in writes:
            self.cells[c] = [ev, {}]

    def op(self, eng, fn, reads=(), writes=()):
        if self.dry:
            return
        reads = self._cells(reads)
        writes = self._cells(writes)
        waits = self._deps(eng, reads, writes)
        sem = self.esem[eng]
        sem.count += 1
        self.q[eng].append((waits, fn, sem, 1))
        self._commit((sem, sem.count), reads, writes)
        self.nops[eng] += 1

    def mm(self, out, pairs, reads, writes):
        if self.dry:
            return
        pairs = list(pairs)

        def fn(t):
            n = len(pairs)
            ins = None
            for i, (l, r) in enumerate(pairs):
                ins = t.matmul(out, l, r, start=(i == 0), stop=(i == n - 1))
            return ins
        self.op('pe', fn, reads, writes)

    def transpose(self, out, in_, ident, reads, writes):
        self.op('pe', lambda t: t.transpose(out, in_, ident), reads, writes)

    def dma(self, q, out, in_, reads, writes, sem):
        if self.dry:
            return
        reads = self._cells(reads)
        writes = self._cells(writes)
        waits = self._deps(q, reads, writes)
        if sem.count and self.seen[q].get(sem, 0) < sem.count:
            self.seen[q][sem] = sem.count
            waits = [w for w in waits if w[0] is not sem] + [(sem, sem.count)]
        sem.count += 16
        self.q[q].append((waits, lambda e: e.dma_start(out=out, in_=in_), sem, 16))
        self._commit((sem, sem.count), reads, writes)
        self.nops[q] += 1

    def dump(self, name, ap, reads):
        if self.dry or not getattr(self, 'debug', False):
            return
        shape = list(ap.shape)
        t = self.nc.dram_tensor("dbg_" + name, shape, F32, kind="ExternalOutput").ap()
        if not hasattr(self, 'dbg_names'):
            self.dbg_names = []
        self.dbg_names.append("dbg_" + name)
        self.dma('pool', t, ap, reads, [], self.new_sem("dbg_" + name))

    def ps(self):
        nres = getattr(self, 'ps_nres', 0)
        nfree = len(self.psum) - nres
        i = nres + (self.ps_rr % nfree)
        self.ps_rr += 1
        return self.psum[i][:, :], ('ps', i)

    def ps_reserve(self, n):
        self.ps_nres = n
        return [(self.psum[i][:, :], ('ps', i)) for i in range(n)]

    def ps_release(self):
        self.ps_nres = 0

    def weights(self, loader):
        if self.dry:
            self.wreq.append(loader)
            return self.wslots[0][0]
        ns = len(self.wslots)
        k = self.wi
        self.wi += 1
        while self.wissued < min(len(self.wreq), k + ns - 1):
            j = self.wissued
            slot, sem = self.wslots[j % ns]
            wc = getattr(self, 'wcache', None)
            if wc is None:
                self.wreq[j](slot, sem)
            else:
                R, n_write, scratch, wb_sems = wc
                p, r = j // R, j % R
                if p >= n_write:
                    self.dma('pool', slot.ap, scratch[r], [('wsc', r)], [slot], sem)
                else:
                    self.wreq[j](slot, sem)
                    if r % n_write == p:
                        self.dma('sp', scratch[r], slot.ap, [slot], [('wsc', r)], wb_sems[j % ns])
            self.wissued += 1
        return self.wslots[k % ns][0]

    def emit(self, final_sems):
        nc = self.nc
        with nc.Block() as block:
            def run(eng, e):
                for waits, fn, sem, inc in self.q[eng]:
                    for ws, v in waits:
                        e.wait_ge(ws.h, v)
                    ins = fn(e)
                    ins.then_inc(sem.h, inc)

            @block.tensor
            def _(t):
                run('pe', t)

            @block.scalar
            def _(a):
                run('act', a)

            @block.vector
            def _(v):
                run('dve', v)

            @block.gpsimd
            def _(g):
                run('pool', g)

            @block.sync
            def _(sp):
                run('sp', sp)
                for s in final_sems:
                    if s.count:
                        sp.wait_ge(s.h, s.count)


KB = 1024


class Ctx:
    pass


def setup(nc, stack, cfg):
    arena_bytes = cfg.get('arena_bytes', 206 * KB)
    arena = stack.enter_context(nc.sbuf_tensor("arena", [128, arena_bytes // 4], F32))
    psum = [stack.enter_context(nc.psum_tensor(f"ps{i}", [128, 512], F32)) for i in range(8)]
    B = Builder(nc, arena, arena_bytes, psum, stack)
    C = Ctx()
    C.cfg = cfg
    D, T = cfg['D'], cfg['T']
    C.NC = D // 128
    NC = C.NC
    TB = T * 4
    C.X0, C.H0, C.Y0, C.A0 = 0, 32 * KB, 48 * KB, 80 * KB
    C.W0 = 124 * KB
    C.M0 = 188 * KB
    C.xT = [Unit(B, C.X0 + c * 2 * KB, TB) for c in range(NC)]
    C.hT = [Unit(B, C.H0 + c * 1 * KB, TB // 2, BF16) for c in range(NC)]
    C.yT = [Unit(B, C.Y0 + c * 2 * KB, TB) for c in range(NC)]
    C.NFC = cfg['DFF'] // 128
    C.actT = [Unit(B, C.A0 + f * 1 * KB, TB // 2, BF16) for f in range(C.NFC)]
    C.sq = [Unit(B, C.A0 + c * 1 * KB, TB // 2, BF16) for c in range(NC)]
    C.stage = [Unit(B, C.A0 + b * 8 * KB, D * 4) for b in range(4)]
    B.wslots = [(Unit(B, C.W0 + s * 16 * KB, 16 * KB, BF16), B.new_sem(f"w{s}")) for s in range(4)]
    m = C.M0

    def alloc(nbytes, dtype=F32):
        nonlocal m
        u = Unit(B, m, nbytes, dtype)
        m += (nbytes + CELL - 1) // CELL * CELL
        return u
    C.alloc = alloc
    C.mptr = lambda: m
    C.ones_bf = alloc(256, BF16)
    C.ident = alloc(512)
    C.rstd = alloc(2 * KB)
    C.tmp = [Unit(B, C.Y0 + i * 2 * KB, 2 * KB) for i in range(2)]
    C.prm = [alloc(1 * KB) for _ in range(cfg['L'])]
    C.pstage = alloc(512)
    C.eps_u = C.ones_bf_pad = None
    C.sem_misc = B.new_sem("misc")
    C.sem_x = B.new_sem("xin")
    C.sem_out = [B.new_sem(f"out{i}") for i in range(4)]
    return B, C


def load_consts(B, C, d):
    B.dma('sp', C.ident.ap, d['c_ident'], [], [C.ident], C.sem_misc)
    B.op('dve', lambda v: v.memset(C.ones_bf.ap, 1.0), [], [C.ones_bf])
    C.eps_u = C.ones_bf
    e32 = B.arena[:, (C.ones_bf.off + 256) // 4:(C.ones_bf.off + 256) // 4 + 2]
    C.epsk = {1.0: e32[:, 0:1], 4.0: e32[:, 1:2]}
    B.op('dve', lambda v: v.memset(e32[:, 0:1], 1e-6), [], [C.ones_bf])
    B.op('dve', lambda v: v.memset(e32[:, 1:2], 4e-6), [], [C.ones_bf])


def load_params(B, C, d, l):
    NC = C.NC
    nj = 6
    src = d['norm_w'][l].rearrange("j (c p) -> (j c) p", p=128)
    st = C.pstage
    B.dma('sp', st.ap[0:nj * NC, 0:128], src, [], [st], C.sem_misc)
    ps, pk = B.ps()
    B.transpose(ps[:, 0:nj * NC], st.ap[0:nj * NC, 0:128], C.ident.ap[0:nj * NC, 0:nj * NC], [st, C.ident], [pk])
    B.op('dve', lambda v: v.tensor_copy(C.prm[l].ap[:, 0:nj * NC], ps[:, 0:nj * NC]), [pk], [C.prm[l]])
    if 'ssd_conv_w' in d:
        B.dma('sp', st.ap[0:32, 0:128], d['ssd_conv_w'][l].rearrange("k (c p) -> (k c) p", p=128), [], [st], C.sem_misc)
        B.dma('sp', st.ap[32:40, 0:128], d['ssd_conv_b'][l].rearrange("(c p) -> c p", p=128), [], [st], C.sem_misc)
        B.dma('sp', st.ap[40:44, 0:128], d['ssd_norm_w'][l].rearrange("(c p) -> c p", p=128), [], [st], C.sem_misc)
        B.dma('sp', st.ap[44:48, 0:128], d['ssd_d_rep'][l].rearrange("(c p) -> c p", p=128), [], [st], C.sem_misc)
        ps2, pk2 = B.ps()
        B.transpose(ps2[:, 0:48], st.ap[0:48, 0:128], C.ident.ap[0:48, 0:48], [st, C.ident], [pk2])
        B.op('dve', lambda v: v.tensor_copy(C.prm[l].ap[:, 96:144], ps2[:, 0:48]), [pk2], [C.prm[l]])


def nw_col(C, l, j, c):
    return C.prm[l].ap[:, j * C.NC + c: j * C.NC + c + 1]


def load_x_tile(B, C, x_rows, T, Treal=None):
    NC = C.NC
    Treal = T if Treal is None else Treal
    nb = (T + 127) // 128
    for b in range(nb):
        n = min(128, T - b * 128)
        r = max(0, min(128, Treal - b * 128))
        if r < n:
            B.op('dve', lambda v, b=b: v.memset(C.stage[b].ap, 0.0), [], [C.stage[b]])
        if r:
            B.dma('sp', C.stage[b].ap[0:r, :], x_rows[b * 128:b * 128 + r, :], [], [C.stage[b]], C.sem_x)
    for c in range(NC):
        ps, pk = B.ps()
        for b in range(nb):
            n = min(128, T - b * 128)
            B.transpose(ps[:, b * 128:b * 128 + n], C.stage[b].ap[0:n, c * 128:(c + 1) * 128],
                        C.ident.ap[0:n, 0:n], [C.stage[b], C.ident], [pk])
        if c % 2 == 0:
            B.op('act', lambda a, c=c, ps=ps: a.copy(C.xT[c].ap[:, 0:T], ps[:, 0:T]), [pk], [C.xT[c]])
        else:
            B.op('dve', lambda v, c=c, ps=ps: v.tensor_copy(C.xT[c].ap[:, 0:T], ps[:, 0:T]), [pk], [C.xT[c]])


def store_x_tile(B, C, y_rows, T):
    NC = C.NC
    nb = (T + 127) // 128
    for b in range(nb):
        n = min(128, T - b * 128)
        st = C.stage[b]
        for g in range(NC // 4):
            ps, pk = B.ps()
            for j in range(4):
                c = g * 4 + j
                B.transpose(ps[0:n, j * 128:(j + 1) * 128], C.xT[c].ap[:, b * 128:b * 128 + n],
                            C.ident.ap, [C.xT[c], C.ident], [pk])
            if g % 2 == 0:
                B.op('act', lambda a, g=g, ps=ps, st=st, n=n: a.copy(st.ap[0:n, g * 512:(g + 1) * 512], ps[0:n, :]), [pk], [st])
            else:
                B.op('dve', lambda v, g=g, ps=ps, st=st, n=n: v.tensor_copy(st.ap[0:n, g * 512:(g + 1) * 512], ps[0:n, :]), [pk], [st])
        B.dma('sp', y_rows[b * 128:b * 128 + n, :], st.ap[0:n, :], [st], [], C.sem_out[b])


def rms_rstd(B, C, src, T, nch, dim, sq_from_src=True, k=1.0):
    if sq_from_src:
        for c in range(nch):
            B.op('act', lambda a, c=c: a.activation(out=C.sq[c].ap[:, 0:T], in_=src[c].ap[:, 0:T], func=AF.Square),
                 [src[c]], [C.sq[c]])
    ps, pk = B.ps()
    B.mm(ps[:, 0:T], [(C.ones_bf.ap, C.sq[c].ap[:, 0:T]) for c in range(nch)], [C.ones_bf] + C.sq[:nch], [pk])
    B.op('act', lambda a: a.activation(out=C.rstd.ap[:, 0:T], in_=ps[:, 0:T], func=AF.Ln, bias=C.epsk[k], scale=k / dim), [pk, C.eps_u], [C.rstd])
    B.op('act', lambda a: a.activation(out=C.rstd.ap[:, 0:T], in_=C.rstd.ap[:, 0:T], func=AF.Exp, scale=-0.5), [C.rstd], [C.rstd])


def norm_to_h(B, C, l, j, T):
    rms_rstd(B, C, C.xT, T, C.NC, C.cfg['D'])
    B.dump(f"rstd_{l}_{j}", C.rstd.ap[:, 0:T], [C.rstd])
    B.dump(f"prm_{l}_{j}", C.prm[l].ap[:, 0:96], [C.prm[l]])
    for c in range(C.NC):
        B.op('dve', lambda v, c=c: v.scalar_tensor_tensor(C.hT[c].ap[:, 0:T], C.xT[c].ap[:, 0:T], nw_col(C, l, j, c),
                                                          C.rstd.ap[:, 0:T], ALU.mult, ALU.mult),
             [C.xT[c], C.rstd, C.prm[l]], [C.hT[c]])


RESID_ENG = 'dve'


def resid_add(B, C, l, j, T, half):
    rms_rstd(B, C, C.yT, T, C.NC, C.cfg['D'], sq_from_src=False, k=(4.0 if half else 1.0))
    for c in range(C.NC):
        B.op('dve', lambda v, c=c: v.scalar_tensor_tensor(C.yT[c].ap[:, 0:T], C.yT[c].ap[:, 0:T], nw_col(C, l, j, c),
                                                          C.rstd.ap[:, 0:T], ALU.mult, ALU.mult),
             [C.yT[c], C.rstd, C.prm[l]], [C.yT[c]])
        B.op(RESID_ENG, lambda g, c=c: g.tensor_tensor(C.xT[c].ap[:, 0:T], C.xT[c].ap[:, 0:T], C.yT[c].ap[:, 0:T], ALU.add),
             [C.xT[c], C.yT[c]], [C.xT[c]])


def ffn(B, C, d, l, i, T):
    NC, NFC = C.NC, C.NFC
    DFF = C.cfg['DFF']
    cut = C.cfg.get('cut', 99)
    if cut < 1:
        return
    norm_to_h(B, C, l, 4 * i, T)
    B.dump(f"h0_{l}_{i}", C.hT[0].ap[:, 0:T], [C.hT[0]])
    B.dump(f"h5_{l}_{i}", C.hT[5].ap[:, 0:T], [C.hT[5]])
    if cut < 2:
        return
    wg = d['w_ffn_gate'][l, i].rearrange("(c p) n -> p c n", p=128)
    wu = d['w_ffn_up'][l, i].rearrange("(c p) n -> p c n", p=128)
    wd = d['w_ffn_down'][l, i].rearrange("(f p) n -> p f n", p=128)
    ngrp = (NFC + 3) // 4
    for g in range(ngrp):
        nf = min(4, NFC - g * 4)
        def loader_g(slot, sem, g=g, nf=nf):
            o = slot.ap[:, 0:NC * 512].rearrange("p (c n) -> p c n", n=512)
            B.dma('pool', o[:, :, 0:nf * 128], wg[:, :, g * 512:g * 512 + nf * 128], [], [slot], sem)
        def loader_u(slot, sem, g=g, nf=nf):
            o = slot.ap[:, 0:NC * 512].rearrange("p (c n) -> p c n", n=512)
            B.dma('pool', o[:, :, 0:nf * 128], wu[:, :, g * 512:g * 512 + nf * 128], [], [slot], sem)
        slot_g = B.weights(loader_g)
        slot_u = B.weights(loader_u)
        sg = slot_g.ap[:, 0:NC * 512].rearrange("p (c n) -> p c n", n=512)
        su = slot_u.ap[:, 0:NC * 512].rearrange("p (c n) -> p c n", n=512)
        if g == 0:
            B.dump(f"wg_{l}_{i}", slot_g.ap[:, 0:1024], [slot_g])
            B.dump(f"wu_{l}_{i}", slot_u.ap[:, 0:1024], [slot_u])
        for j in range(nf):
            f = g * 4 + j
            pa, ka = B.ps()
            pg, kg = B.ps()
            B.mm(pa[:, 0:T], [(sg[:, c, j * 128:(j + 1) * 128], C.hT[c].ap[:, 0:T]) for c in range(NC)], [slot_g] + C.hT, [ka])
            B.mm(pg[:, 0:T], [(su[:, c, j * 128:(j + 1) * 128], C.hT[c].ap[:, 0:T]) for c in range(NC)], [slot_u] + C.hT, [kg])
            tmp = C.tmp[f % 2]
            B.op('act', lambda a, pa=pa, tmp=tmp: a.activation(out=tmp.ap[:, 0:T], in_=pa[:, 0:T], func=AF.Silu), [ka], [tmp])
            if f == 1:
                B.dump(f"silu1_{l}_{i}", tmp.ap[:, 0:T], [tmp])
            B.op('dve', lambda v, pg=pg, tmp=tmp, f=f: v.tensor_tensor(C.actT[f].ap[:, 0:T], tmp.ap[:, 0:T], pg[:, 0:T], ALU.mult),
                 [kg, tmp], [C.actT[f]])
    B.dump(f"act1_{l}_{i}", C.actT[1].ap[:, 0:T], [C.actT[1]])
    if cut < 3:
        return
    for p4 in range(NC // 4):
        banks = [B.ps() for _ in range(4)]
        nsl = (NFC + 15) // 16
        for sidx in range(nsl):
            f0 = sidx * 16
            nf = min(16, NFC - f0)
            def loader_d(slot, sem, f0=f0, nf=nf, p4=p4):
                o = slot.ap[:, 0:16 * 512].rearrange("p (f n) -> p f n", n=512)
                B.dma('pool', o[:, 0:nf, :], wd[:, f0:f0 + nf, p4 * 512:(p4 + 1) * 512], [], [slot], sem)
            slot = B.weights(loader_d)
            sd = slot.ap[:, 0:16 * 512].rearrange("p (f n) -> p f n", n=512)
            for j in range(4):
                py, ky = banks[j]
                def fn(t, py=py, sd=sd, j=j, f0=f0, nf=nf, sidx=sidx, nsl=nsl):
                    ins = None
                    for ff in range(nf):
                        ins = t.matmul(py[:, 0:T], sd[:, ff, j * 128:(j + 1) * 128], C.actT[f0 + ff].ap[:, 0:T],
                                       start=(sidx == 0 and ff == 0), stop=(sidx == nsl - 1 and ff == nf - 1))
                    return ins
                if C.cfg.get('var', 0) != 2:
                    B.op('pe', fn, [slot] + C.actT[f0:f0 + nf], [ky])
        for j in range(4):
            c = p4 * 4 + j
            py, ky = banks[j]
            if C.cfg.get('var', 0) == 1:
                continue
            B.op('dve', lambda v, c=c, py=py: v.tensor_copy(C.yT[c].ap[:, 0:T], py[:, 0:T]), [ky], [C.yT[c]])
            B.op('act', lambda a, c=c: a.activation(out=C.hT[c].ap[:, 0:T], in_=C.yT[c].ap[:, 0:T], func=AF.Square), [C.yT[c]], [C.hT[c]])
    B.dump(f"y3_{l}_{i}", C.yT[3].ap[:, 0:T], [C.yT[3]])
    if cut < 4:
        return
    sq_save = C.sq
    C.sq = C.hT
    resid_add(B, C, l, 4 * i + 1, T, half=True)
    C.sq = sq_save


ATTN_SCALE = 128 ** -0.5
GELU_K = 1.5957691216057308


class Stream:
    def __init__(self, L):
        self.past = [[] for _ in range(L)]
        self.npast_c = [0] * L
        self.k_out = self.v_out = self.lf_out = self.gm_out = self.conv_out = self.state_out = None
        self.tok0 = 0
        self.last = True


def setup_mixer(B, C):
    A, Y = C.A0, C.Y0
    bf1 = lambda off: Unit(B, off, 1 * KB, BF16)
    C.mixedT = [bf1(A + 28 * KB + i * KB) for i in range(16)]
    C.qT = [bf1(A + i * KB) for i in range(8)]
    C.kT = [bf1(A + 8 * KB + i * KB) for i in range(8)]
    C.vbf = [Unit(B, A + 16 * KB + b * 2 * KB, 2 * KB, BF16) for b in range(4)]
    C.pT = [bf1(A + 24 * KB + i * KB) for i in range(3)]
    C.kstage = [Unit(B, Y + b * 4 * KB, 4 * KB) for b in range(4)]
    C.vstage = [Unit(B, Y + 16 * KB + b * 4 * KB, 4 * KB) for b in range(4)]
    NR = 6
    C.NR = NR
    C.kblk = [Unit(B, Y + i * 512, 512) for i in range(NR)]
    C.kTblk = [Unit(B, Y + 3 * KB + i * 512, 256, BF16) for i in range(NR)]
    C.vblk = [Unit(B, Y + 6 * KB + i * 512, 256, BF16) for i in range(NR)]
    C.rc = Unit(B, Y + 9 * KB, 2 * KB)
    C.accs = Unit(B, Y + 11 * KB, 2 * KB)
    C.cqbc = [Unit(B, Y + 13 * KB + h * KB, 1 * KB, BF16) for h in range(8)]


def setup_small(B, C, L):
    al = C.alloc
    C.U_f = al(512)
    C.U_bf = al(256, BF16)
    C.ones_f = al(512)
    C.negm = al(512)
    C.ST = [al(2 * KB) for _ in range(L)]
    C.c_all = [al(1 * KB) for _ in range(L)]
    C.negc = al(1 * KB)
    C.convst = [al(512) for _ in range(L)]
    C.small = [al(512) for _ in range(L)]
    C.lf = al(512)
    C.fdtu = al(512)
    C.wfd = [al(512, BF16) for _ in range(L)]
    C.identbf = al(256, BF16)
    C.cq = C.rstd


def load_consts2(B, C, d):
    B.dma('sp', C.U_f.ap, d['c_tri'], [], [C.U_f], C.sem_misc)
    B.op('dve', lambda v: v.tensor_copy(C.U_bf.ap, C.U_f.ap), [C.U_f], [C.U_bf])
    B.op('dve', lambda v: v.memset(C.ones_f.ap, 1.0), [], [C.ones_f])
    B.op('dve', lambda v: v.tensor_copy(C.identbf.ap, C.ident.ap), [C.ident], [C.identbf])
    B.op('dve', lambda v: v.tensor_scalar(C.negm.ap, C.U_f.ap, -1.0, 30000.0, ALU.add, ALU.mult), [C.U_f], [C.negm])


def load_layer_small(B, C, d, l):
    sm = C.small[l]
    B.dma('sp', sm.ap[:, 0:8], d['fox_fb'][l:l + 1, :].partition_broadcast(128), [], [sm], C.sem_misc)
    B.dma('sp', sm.ap[:, 8:16], d['ssd_dt_bias'][l:l + 1, :].partition_broadcast(128), [], [sm], C.sem_misc)
    B.dma('sp', sm.ap[:, 16:24], d['ssd_a_log'][l:l + 1, :].partition_broadcast(128), [], [sm], C.sem_misc)
    win = d['w_in'][l].rearrange("(c p) n -> p c n", p=128)
    w3 = C.wfd[l].ap.rearrange("p (c k) -> p c k", k=16)
    B.dma('pool', w3[:, :, 0:8], win[:, :, 3072:3080], [], [C.wfd[l]], C.sem_wfd)
    B.dma('pool', w3[:, :, 8:16], win[:, :, 5640:5648], [], [C.wfd[l]], C.sem_wfd)
    B.op('act', lambda a: a.activation(out=sm.ap[:, 16:24], in_=sm.ap[:, 16:24], func=AF.Exp), [sm], [sm])
    B.op('dve', lambda v: v.tensor_scalar(sm.ap[:, 16:24], sm.ap[:, 16:24], -1.0, None, ALU.mult), [sm], [sm])


def slot3(slot):
    return slot.ap.rearrange("p (c n) -> p c n", n=512)


def proj_fm(B, C, win, col0, ncols, T, evac):
    def loader(slot, sem):
        B.dma('pool', slot3(slot)[:, :, 0:ncols], win[:, :, col0:col0 + ncols], [], [slot], sem)
    slot = B.weights(loader)
    s3 = slot3(slot)
    for j in range(ncols // 128):
        ps, pk = B.ps()
        B.mm(ps[:, 0:T], [(s3[:, c, j * 128:(j + 1) * 128], C.hT[c].ap[:, 0:T]) for c in range(C.NC)], [slot] + C.hT, [pk])
        evac(j, ps, pk)


def proj_tm(B, C, win, cols, T, evac):
    tot = sum(n for _, n in cols)

    def loader(slot, sem):
        o = 0
        for c0, n in cols:
            B.dma('pool', slot3(slot)[:, :, o:o + n], win[:, :, c0:c0 + n], [], [slot], sem)
            o += n
    slot = B.weights(loader)
    s3 = slot3(slot)
    nb = (T + 127) // 128
    for b in range(nb):
        n = min(128, T - b * 128)
        ps, pk = B.ps()
        B.mm(ps[0:n, 0:tot], [(C.hT[c].ap[:, b * 128:b * 128 + n], s3[:, c, 0:tot]) for c in range(C.NC)], [slot] + C.hT, [pk])
        evac(b, n, ps, pk)


def cumsum_blocks(B, C, l, src_of_block, nblocks, sizes, kb0):
    sm = C.small[l]
    ca = C.c_all[l].ap[:, 0:17 * 8].rearrange("p (k h) -> p k h", h=8)
    for b in range(nblocks):
        n = sizes[b]
        src, res = src_of_block(b)
        ps, pk = B.ps()
        B.mm(ps[0:n, 0:8], [(C.U_f.ap[0:n, 0:n], src)], [C.U_f] + res, [pk])
        B.mm(ps[:, 8:16], [(C.ones_f.ap[0:n, :], src)], [C.ones_f] + res, [pk])
        B.op('dve', lambda v, ps=ps, n=n, b=b: v.tensor_tensor(ca[0:n, kb0 + b, :], ps[0:n, 0:8], sm.ap[0:n, 24:32], ALU.add),
             [pk, sm], [C.c_all[l]])
        B.op('dve', lambda v, ps=ps: v.tensor_tensor(sm.ap[:, 24:32], ps[:, 8:16], sm.ap[:, 24:32], ALU.add), [pk, sm], [sm])


def fox_phase(B, C, d, l, T, S, win, Treal):
    nb = (T + 127) // 128
    bn = [min(128, T - b * 128) for b in range(nb)]
    bo = [max(0, min(128, Treal - b * 128)) for b in range(nb)]
    sm = C.small[l]
    for half in range(2):
        def ev_q(j, ps, pk, half=half):
            B.op('act', lambda a: a.mul(C.qT[half * 4 + j].ap[:, 0:T], ps[:, 0:T], ATTN_SCALE), [pk], [C.qT[half * 4 + j]])
        proj_fm(B, C, win, half * 512, 512, T, ev_q)
    for half in range(2):
        def ev_k(j, ps, pk, half=half):
            B.op('dve', lambda v: v.tensor_copy(C.kT[half * 4 + j].ap[:, 0:T], ps[:, 0:T]), [pk], [C.kT[half * 4 + j]])
        proj_fm(B, C, win, 1024 + half * 512, 512, T, ev_k)
    for b in range(nb):
        n = bn[b]
        for half in range(2):
            ps, pk = B.ps()
            psb = ps.bitcast(BF16)
            for j in range(4):
                h = half * 4 + j
                B.transpose(psb[0:n, j * 128:(j + 1) * 128], C.kT[h].ap[:, b * 128:b * 128 + n], C.identbf.ap, [C.kT[h], C.identbf], [pk])
            B.op('act', lambda a, b=b, n=n, half=half, psb=psb: a.copy(C.kstage[b].ap[0:n, half * 512:(half + 1) * 512], psb[0:n, 0:512]),
                 [pk], [C.kstage[b]])
    for b in range(nb):
        n = bo[b]
        if n == 0:
            continue
        B.dma('sp', S.k_out[l][S.tok0 + b * 128:S.tok0 + b * 128 + n, :], C.kstage[b].ap[0:n, :], [C.kstage[b]],
              [('kout', id(S), l, S.tok0 // 128 + b)], C.sem_kv[b])
    for half in range(2):
        def ev_vt(b, n, ps, pk, half=half):
            B.op('act', lambda a: a.copy(C.vstage[b].ap[0:n, half * 512:(half + 1) * 512], ps[0:n, 0:512]), [pk], [C.vstage[b]])
            B.op('dve', lambda v: v.tensor_copy(C.vbf[b].ap[0:n, half * 512:(half + 1) * 512],
                                                C.vstage[b].ap[0:n, half * 512:(half + 1) * 512]), [C.vstage[b]], [C.vbf[b]])
        proj_tm(B, C, win, [(2048 + half * 512, 512)], T, ev_vt)
    for b in range(nb):
        n = bo[b]
        if n == 0:
            continue
        B.dma('sp', S.v_out[l][S.tok0 + b * 128:S.tok0 + b * 128 + n, :], C.vstage[b].ap[0:n, :], [C.vstage[b]],
              [('vout', id(S), l, S.tok0 // 128 + b)], C.sem_kv[4 + b])
    fd = C.fdtu.ap[:, 0:64].rearrange("p (b k) -> p b k", k=16)

    w3 = C.wfd[l].ap.rearrange("p (c k) -> p c k", k=16)
    for b in range(nb):
        n = bn[b]
        ps, pk = B.ps()
        B.mm(ps[0:n, 0:16], [(C.hT[c].ap[:, b * 128:b * 128 + n], w3[:, c, :]) for c in range(C.NC)], [C.wfd[l]] + C.hT, [pk])
        B.op('dve', lambda v, b=b, n=n, ps=ps: v.tensor_copy(fd[0:n, b, :], ps[0:n, 0:16]), [pk], [C.fdtu])
    lf = C.lf.ap[:, 0:32].rearrange("p (b h) -> p b h", h=8)
    dt = C.lf.ap[:, 32:64].rearrange("p (b h) -> p b h", h=8)
    dtA = C.lf.ap[:, 64:96].rearrange("p (b h) -> p b h", h=8)
    for b in range(nb):
        n = bn[b]
        B.op('dve', lambda v, b=b, n=n: v.tensor_tensor(lf[0:n, b, :], fd[0:n, b, 0:8], sm.ap[0:n, 0:8], ALU.add), [C.fdtu, sm], [C.lf])
        B.op('dve', lambda v, b=b, n=n: v.tensor_tensor(dt[0:n, b, :], fd[0:n, b, 8:16], sm.ap[0:n, 8:16], ALU.add), [C.fdtu, sm], [C.lf])
    nbh = nb * 8
    P = 128 if T >= 128 else T
    B.op('act', lambda a: a.activation(out=C.lf.ap[0:P, 0:nbh], in_=C.lf.ap[0:P, 0:nbh], func=AF.Sigmoid), [C.lf], [C.lf])
    B.op('act', lambda a: a.activation(out=C.lf.ap[0:P, 0:nbh], in_=C.lf.ap[0:P, 0:nbh], func=AF.Ln), [C.lf], [C.lf])
    B.op('act', lambda a: a.activation(out=C.lf.ap[0:P, 32:32 + nbh], in_=C.lf.ap[0:P, 32:32 + nbh], func=AF.Exp), [C.lf], [C.lf])
    B.op('dve', lambda v: v.tensor_scalar(C.lf.ap[0:P, 32:32 + nbh], C.lf.ap[0:P, 32:32 + nbh], 1.0, None, ALU.add), [C.lf], [C.lf])
    B.op('act', lambda a: a.activation(out=C.lf.ap[0:P, 32:32 + nbh], in_=C.lf.ap[0:P, 32:32 + nbh], func=AF.Ln), [C.lf], [C.lf])
    for b in range(nb):
        n = bn[b]
        if bo[b] < n:
            assert bo[b] == 32 and n == 128
            B.op('dve', lambda v, b=b: v.memset(dt[32:64, b, :], 0.0), [C.lf], [C.lf])
            B.op('dve', lambda v, b=b: v.memset(dt[64:128, b, :], 0.0), [C.lf], [C.lf])
        B.op('dve', lambda v, b=b, n=n: v.tensor_tensor(dtA[0:n, b, :], dt[0:n, b, :], sm.ap[0:n, 16:24], ALU.mult), [C.lf, sm], [C.lf])
        if bo[b]:
            B.dma('sp', S.lf_out[l][S.tok0 + b * 128:S.tok0 + b * 128 + bo[b], :], lf[0:bo[b], b, :], [C.lf], [], C.sem_lf)
    kb0 = S.npast_c[l]
    cref = sm.ap[:, 32:40]
    B.op('dve', lambda v: v.tensor_copy(cref, sm.ap[:, 24:32]), [sm], [sm])
    cumsum_blocks(B, C, l, lambda b: (lf[0:bn[b], b, :], [C.lf]), nb, bn, kb0)
    nk = kb0 + nb
    ca = C.c_all[l].ap[:, 0:17 * 8].rearrange("p (k h) -> p k h", h=8)
    ng3 = C.negc.ap[:, 0:17 * 8].rearrange("p (k h) -> p k h", h=8)
    B.op('dve', lambda v: v.tensor_tensor(ng3[:, 0:nk, :], cref.unsqueeze(1).to_broadcast([128, nk, 8]), ca[:, 0:nk, :], ALU.subtract),
         [C.c_all[l], sm], [C.negc])
    chi = C.fdtu.ap[:, 64:64 + 16].bitcast(BF16).rearrange("p (b h) -> p b h", h=8)
    B.op('dve', lambda v: v.tensor_scalar(chi[:, 0:nb, :], ng3[:, kb0:kb0 + nb, :], -1.0, None, ALU.mult), [C.negc], [C.fdtu])
    ps, pk = B.ps()
    psb = ps.bitcast(BF16)
    for b in range(nb):
        n = bn[b]
        B.transpose(psb[0:8, b * 128:b * 128 + n], chi[0:n, b, :], C.identbf.ap[0:n, 0:n], [C.fdtu, C.identbf], [pk])
    cqb = C.cq.ap[0:8, 0:256].bitcast(BF16)
    B.op('act', lambda a, psb=psb: a.copy(cqb[0:8, 0:T], psb[0:8, 0:T]), [pk], [C.cq])
    blocks = [('past', kap, vap, key, n) for (kap, vap, key, n) in S.past[l]] + [('own', b) for b in range(nb)]
    accs = B.ps_reserve(4)
    nblk = len(blocks)
    units = [(h, bi) for h in range(8) for bi in range(nblk)]
    LA = 2
    stash = {}

    def head_begin(h):
        ps, pk = B.ps()
        B.mm(ps[:, 0:T], [(C.identbf.ap[0:8, h:h + 1].to_broadcast([8, 128]), cqb[0:8, 0:T])], [C.identbf, C.cq], [pk])
        B.op('act', lambda a, ps=ps, h=h: a.copy(C.cqbc[h].ap[:, 0:T], ps[:, 0:T]), [pk], [C.cqbc[h]])

    def stage_a(u):
        h, bi = units[u]
        blk = blocks[bi]
        if blk[0] == 'past':
            _, kap, vap, key, n = blk
            kst, kTb, vb = C.kblk[u % C.NR], C.kTblk[u % C.NR], C.vblk[u % C.NR]
            B.dma('sp', kst.ap[0:n, 0:128], kap[:, h * 128:(h + 1) * 128], [key[0]], [kst], C.sem_kb[u % C.NR])
            pst, pstk = B.ps()
            B.transpose(pst[:, 0:n], kst.ap[0:n, 0:128], C.ident.ap[0:n, 0:n], [kst, C.ident], [pstk])
            B.op('act', lambda a, kTb=kTb, pst=pst, n=n: a.copy(kTb.ap[:, 0:n], pst[:, 0:n]), [pstk], [kTb])
            B.dma('pool', vb.ap[0:n, 0:128], vap[:, h * 128:(h + 1) * 128], [key[1]], [vb], C.sem_vb[u % C.NR])
            lhsK, lhsV, q0 = kTb.ap[:, 0:n], vb.ap[0:n, 0:128], 0
            rK, rV = [kTb], [vb]
            ncol = bi * 8 + h
            own = False
        else:
            b = blk[1]
            n = bn[b]
            lhsK, lhsV, q0 = C.kT[h].ap[:, b * 128:b * 128 + n], C.vbf[b].ap[0:n, h * 128:(h + 1) * 128], b * 128
            rK, rV = [C.kT[h]], [C.vbf[b]]
            ncol = (kb0 + b) * 8 + h
            own = True
        pss, pssk = B.ps()
        B.mm(pss[0:n, q0:T], [(lhsK, C.qT[h].ap[:, q0:T])], rK + [C.qT[h]], [pssk])
        pT = C.pT[u % 3]
        B.op('dve', lambda v, pss=pss, n=n, q0=q0, h=h: v.tensor_tensor(pss[0:n, q0:T], pss[0:n, q0:T], C.cqbc[h].ap[0:n, q0:T], ALU.add),
             [pssk, C.cqbc[h]], [pssk])
        if own:
            B.op('dve', lambda v, pss=pss, n=n, q0=q0: v.tensor_tensor(pss[0:n, q0:q0 + n], pss[0:n, q0:q0 + n],
                                                                       C.negm.ap[0:n, 0:n], ALU.add), [pssk, C.negm], [pssk])
        B.op('act', lambda a, pT=pT, pss=pss, n=n, q0=q0, ncol=ncol: a.activation(
            out=pT.ap[0:n, q0:T], in_=pss[0:n, q0:T], func=AF.Exp, bias=C.negc.ap[0:n, ncol:ncol + 1], scale=1.0),
            [pssk, C.negc], [pT])
        stash[u] = (lhsV, rV, pT, n, q0)

    def stage_b(u):
        h, bi = units[u]
        lhsV, rV, pT, n, q0 = stash.pop(u)
        (po, pok), (pm, pmk) = accs[(h % 2) * 2], accs[(h % 2) * 2 + 1]
        B.op('pe', lambda t, po=po, lhsV=lhsV, pT=pT, n=n, q0=q0, first=(bi == 0), last=(bi == nblk - 1):
             t.matmul(po[:, q0:T], lhsV, pT.ap[0:n, q0:T], start=first, stop=last), rV + [pT], [pok])
        if bi == 0:
            B.op('dve', lambda v, pT=pT: v.tensor_copy(C.accs.ap[:, 0:T], pT.ap[:, 0:T]), [pT], [C.accs])
        else:
            B.op('dve', lambda v, pT=pT, n=n, q0=q0: v.tensor_tensor(C.accs.ap[0:n, q0:T], C.accs.ap[0:n, q0:T], pT.ap[0:n, q0:T], ALU.add),
                 [pT, C.accs], [C.accs])
        if bi == nblk - 1:
            B.mm(pm[:, 0:T], [(C.ones_f.ap, C.accs.ap[:, 0:T])], [C.ones_f, C.accs], [pmk])
            B.op('act', lambda a, pm=pm: a.activation(out=C.rc.ap[:, 0:T], in_=pm[:, 0:T], func=AF.Ln), [pmk], [C.rc])
            B.op('act', lambda a: a.activation(out=C.rc.ap[:, 0:T], in_=C.rc.ap[:, 0:T], func=AF.Exp, scale=-1.0), [C.rc], [C.rc])
            B.op('dve', lambda v, po=po, h=h: v.tensor_tensor(C.mixedT[h].ap[:, 0:T], po[:, 0:T], C.rc.ap[:, 0:T], ALU.mult),
                 [pok, C.rc], [C.mixedT[h]])
    for h in range(8):
        head_begin(h)
    for u in range(len(units) + LA):
        if u < len(units):
            stage_a(u)
        if u - LA >= 0:
            stage_b(u - LA)
    B.ps_release()


def setup_mixer2(B, C):
    A, Y = C.A0, C.Y0
    bf1 = lambda off: Unit(B, off, 1 * KB, BF16)
    C.uT = [bf1(A + i * KB) for i in range(4)]
    C.gtmp = [Unit(B, A + 4 * KB + i * 2 * KB, 2 * KB) for i in range(2)]
    C.gbuf = [Unit(B, Y + i * 2 * KB, 2 * KB) for i in range(2)]
    C.vbfg = [bf1(A + 8 * KB + b * KB) for b in range(4)]
    C.lnw = Unit(B, A + 12 * KB, 2 * KB)
    C.lnb = Unit(B, A + 14 * KB, 2 * KB)
    C.bsb = Unit(B, A + 16 * KB, 2 * KB)
    C.wmT = bf1(A + 18 * KB)
    C.wst = [Unit(B, A + 19 * KB + g * 512, 512) for g in range(4)]
    C.gst = Unit(B, A + 21 * KB, 512)
    C.zs = [bf1(A + i * KB) for i in range(4)]
    C.xpad = Unit(B, A + 4 * KB, 20 * KB)
    C.STbf = bf1(A + 24 * KB)
    C.cvstage = Unit(B, A + 26 * KB, 2 * KB)
    C.xact = [bf1(Y + i * KB) for i in range(8)]
    C.ysT = [Unit(B, Y + 8 * KB + i * 2 * KB, 2 * KB) for i in range(4)]
    C.Rb = Unit(B, Y + 16 * KB, 4 * KB)
    C.ctmp = [Unit(B, Y + 16 * KB + i * 2 * KB, 2 * KB) for i in range(2)]
    C.Lb = Unit(B, Y + 20 * KB, 4 * KB)
    C.Wb = Unit(B, Y + 24 * KB, 2 * KB, BF16)
    C.Cs = Unit(B, Y + 26 * KB, 2 * KB, BF16)
    C.xtm = bf1(Y + 28 * KB)
    C.xw = bf1(Y + 29 * KB)
    C.Btm = Unit(B, Y + 30 * KB, 512, BF16)
    C.cbm = Unit(B, Y + 30 * KB + 512, 1 * KB)
    C.ssm = Unit(B, Y + 31 * KB + 512, 512)
    C.sq4 = [bf1(Y + i * KB) for i in range(4)]


def gelu_ps(B, ps_ap, pk, out_ap, out_res, tmp, tmp_ap):
    B.op('act', lambda a: a.activation(out=tmp_ap, in_=ps_ap, func=AF.Square), [pk], [tmp])
    B.op('dve', lambda v: v.tensor_scalar(tmp_ap, tmp_ap, 0.044715, 1.0, ALU.mult, ALU.add), [tmp], [tmp])
    B.op('dve', lambda v: v.tensor_tensor(tmp_ap, tmp_ap, ps_ap, ALU.mult), [tmp, pk], [tmp])
    B.op('act', lambda a: a.activation(out=tmp_ap, in_=tmp_ap, func=AF.Sigmoid, scale=GELU_K), [tmp], [tmp])
    B.op('dve', lambda v: v.tensor_tensor(out_ap, tmp_ap, ps_ap, ALU.mult), [tmp, pk], out_res)


def gm_phase(B, C, d, l, T, S, win, Treal):
    nb = (T + 127) // 128
    bn = [min(128, T - b * 128) for b in range(nb)]
    bo = [max(0, min(128, Treal - b * 128)) for b in range(nb)]
    B.dma('sp', C.lnw.ap[:, 0:512], d['gm_ln_w'][l:l + 1, :].partition_broadcast(128), [], [C.lnw], C.sem_misc)
    B.dma('sp', C.lnb.ap[:, 0:512], d['gm_ln_b'][l:l + 1, :].partition_broadcast(128), [], [C.lnb], C.sem_misc)
    B.dma('sp', C.bsb.ap[:, 0:512], d['gm_bs'][l:l + 1].rearrange("o g t -> o (g t)").partition_broadcast(128), [], [C.bsb], C.sem_misc)
    ps, pk = B.ps()
    for g in range(4):
        B.dma('sp', C.wst[g].ap[:, 0:128], d['gm_ws'][l, g], [], [C.wst[g]], C.sem_misc)
        B.transpose(ps[:, g * 128:(g + 1) * 128], C.wst[g].ap[:, 0:128], C.ident.ap, [C.wst[g], C.ident], [pk])
    wm3 = C.wmT.ap.rearrange("p (g t) -> p g t", t=128)
    B.op('dve', lambda v, ps=ps: v.tensor_tensor(wm3, ps[:, 0:512].rearrange("p (g t) -> p g t", t=128),
                                          C.U_f.ap.unsqueeze(1).to_broadcast([128, 4, 128]), ALU.mult), [pk, C.U_f], [C.wmT])

    def ev_u(j, ps, pk):
        gelu_ps(B, ps[:, 0:T], pk, C.uT[j].ap[:, 0:T], [C.uT[j]], C.gtmp[j % 2], C.gtmp[j % 2].ap[:, 0:T])
    proj_fm(B, C, win, 3080, 512, T, ev_u)

    def ev_v(b, n, ps, pk):
        tmp, g = C.gtmp[b % 2], C.gbuf[b % 2]
        ga = g.ap[0:n, 0:512]
        st = C.gst.ap
        gelu_ps(B, ps[0:n, 0:512], pk, ga, [g], tmp, tmp.ap[0:n, 0:512])
        B.op('dve', lambda v: v.reduce_sum(st[0:n, 0:1], ga, AX.X), [g], [C.gst])
        B.op('dve', lambda v: v.tensor_scalar(st[0:n, 0:1], st[0:n, 0:1], -1.0 / 512, None, ALU.mult), [C.gst], [C.gst])
        B.op('dve', lambda v: v.tensor_scalar(ga, ga, st[0:n, 0:1], None, ALU.add), [g, C.gst], [g])
        B.op('act', lambda a: a.activation(out=tmp.ap[0:n, 0:512], in_=ga, func=AF.Square, accum_out=st[0:n, 1:2]), [g], [tmp, C.gst])
        B.op('dve', lambda v: v.tensor_scalar(st[0:n, 1:2], st[0:n, 1:2], 1.0 / 512, 1e-6, ALU.mult, ALU.add), [C.gst], [C.gst])
        B.op('act', lambda a: a.activation(out=st[0:n, 1:2], in_=st[0:n, 1:2], func=AF.Sqrt), [C.gst], [C.gst])
        B.op('dve', lambda v: v.reciprocal(st[0:n, 1:2], st[0:n, 1:2]), [C.gst], [C.gst])
        B.op('dve', lambda v: v.scalar_tensor_tensor(ga, ga, st[0:n, 1:2], C.lnw.ap[0:n, 0:512], ALU.mult, ALU.mult), [g, C.gst, C.lnw], [g])
        B.op('dve', lambda v: v.tensor_tensor(ga, ga, C.lnb.ap[0:n, 0:512], ALU.add), [g, C.lnb], [g])
        if S.gm_out is not None and bo[b]:
            B.dma('sp', S.gm_out[l][S.tok0 + b * 128:S.tok0 + b * 128 + bo[b], :], g.ap[0:bo[b], 0:512], [g], [], C.sem_gm[b % 2])
        B.op('act', lambda a: a.copy(C.vbfg[b].ap[0:n, 0:512], ga), [g], [C.vbfg[b]])
    proj_tm(B, C, win, [(3592, 512)], T, ev_v)
    bs3 = C.bsb.ap[:, 0:512].rearrange("p (g t) -> p g t", t=128)
    for g in range(4):
        ps, pk = B.ps()
        for b in range(nb):
            n = bn[b]
            B.mm(ps[:, b * 128:b * 128 + n], [(C.vbfg[b].ap[0:n, g * 128:(g + 1) * 128], wm3[0:n, g, 0:n])], [C.vbfg[b], C.wmT], [pk])
        tmp = C.gtmp[g % 2]
        if T >= 128:
            o3 = tmp.ap[:, 0:T].rearrange("p (b t) -> p b t", t=128)
            i3 = ps[:, 0:T].rearrange("p (b t) -> p b t", t=128)
            bb = bs3[:, g, :].unsqueeze(1).to_broadcast([128, nb, 128])
        else:
            o3, i3, bb = tmp.ap[:, 0:T], ps[:, 0:T], bs3[:, g, 0:T]
        B.op('dve', lambda v, o3=o3, i3=i3, bb=bb: v.tensor_tensor(o3, i3, bb, ALU.add), [pk, C.bsb], [tmp])
        B.op('dve', lambda v, g=g, tmp=tmp: v.tensor_tensor(C.mixedT[8 + g].ap[:, 0:T], tmp.ap[:, 0:T], C.uT[g].ap[:, 0:T], ALU.mult),
             [tmp, C.uT[g]], [C.mixedT[8 + g]])


def ssd_phase(B, C, d, l, T, S, win, Treal):
    nb = (T + 127) // 128
    bn = [min(128, T - b * 128) for b in range(nb)]
    prm = C.prm[l].ap
    cw = lambda k, c: prm[:, 96 + k * 8 + c:96 + k * 8 + c + 1]
    cb = lambda c: prm[:, 128 + c:129 + c]
    nwc = lambda ci: prm[:, 136 + ci:137 + ci]
    dcol = lambda ci: prm[:, 140 + ci:141 + ci]
    xp3 = C.xpad.ap.rearrange("p (c n) -> p c n", n=640)
    cs3 = C.convst[l].ap[:, 0:24].rearrange("p (c k) -> p c k", k=3)
    dt = C.lf.ap[:, 32:64].rearrange("p (b h) -> p b h", h=8)
    dtA = C.lf.ap[:, 64:96].rearrange("p (b h) -> p b h", h=8)
    def ev_z(j, ps, pk):
        B.op('act', lambda a: a.activation(out=C.zs[j].ap[:, 0:T], in_=ps[:, 0:T], func=AF.Silu), [pk], [C.zs[j]])
    proj_fm(B, C, win, 4104, 512, T, ev_z)
    B.op('dve', lambda v: v.tensor_copy(xp3[:, :, 0:3], cs3), [C.convst[l]], [C.xpad])
    for half in range(2):
        def ev_x(j, ps, pk, half=half):
            c = half * 4 + j
            if j % 2 == 0:
                B.op('act', lambda a: a.copy(xp3[:, c, 3:3 + T], ps[:, 0:T]), [pk], [C.xpad])
            else:
                B.op('dve', lambda v: v.tensor_copy(xp3[:, c, 3:3 + T], ps[:, 0:T]), [pk], [C.xpad])
        proj_fm(B, C, win, 4616 + half * 512, 512, T, ev_x)
    for c in range(8):
        tmp = C.ctmp[c % 2]
        ta = tmp.ap[:, 0:T]
        B.op('dve', lambda v, c=c, ta=ta: v.tensor_scalar(ta, xp3[:, c, 0:T], cw(0, c), cb(c), ALU.mult, ALU.add), [C.xpad, C.prm[l]], [tmp])
        for k in range(1, 4):
            B.op('dve', lambda v, c=c, ta=ta, k=k: v.scalar_tensor_tensor(ta, xp3[:, c, k:k + T], cw(k, c), ta, ALU.mult, ALU.add),
                 [C.xpad, C.prm[l], tmp], [tmp])
        B.op('act', lambda a, c=c, ta=ta: a.activation(out=C.xact[c].ap[:, 0:T], in_=ta, func=AF.Silu), [tmp], [C.xact[c]])
    B.op('dve', lambda v: v.tensor_copy(cs3, xp3[:, :, Treal:Treal + 3]), [C.xpad], [C.convst[l]])
    if S.last:
        ps, pk = B.ps()
        ps2, pk2 = B.ps()
        for c in range(8):
            pp, ppk = (ps, pk) if c < 4 else (ps2, pk2)
            B.transpose(pp[0:3, (c % 4) * 128:(c % 4 + 1) * 128], cs3[:, c, :], C.ident.ap, [C.convst[l], C.ident], [ppk])
        B.op('dve', lambda v, ps=ps: v.tensor_copy(C.cvstage.ap[0:3, 0:512], ps[0:3, 0:512]), [pk], [C.cvstage])
        B.dma('sp', S.conv_out[l][:, 0:512], C.cvstage.ap[0:3, 0:512], [C.cvstage], [], C.sem_cv)
        B.op('dve', lambda v, ps2=ps2: v.tensor_copy(C.cvstage.ap[0:3, 0:512], ps2[0:3, 0:512]), [pk2], [C.cvstage])
        B.dma('sp', S.conv_out[l][:, 512:1024], C.cvstage.ap[0:3, 0:512], [C.cvstage], [], C.sem_cv)
    B.op('act', lambda a: a.copy(C.STbf.ap, C.ST[l].ap), [C.ST[l]], [C.STbf])
    ST3 = C.ST[l].ap.rearrange("p (h q) -> p h q", q=64)
    R3 = C.Rb.ap.rearrange("p (h t) -> p h t", t=128)
    L3 = C.Lb.ap.rearrange("p (h t) -> p h t", t=128)
    W3 = C.Wb.ap.rearrange("p (h t) -> p h t", t=128)
    Cs3 = C.Cs.ap.rearrange("p (h t) -> p h t", t=128)
    cbm3 = C.cbm.ap.rearrange("p (g t) -> p g t", t=128)
    ssm = C.ssm.ap
    for b in range(nb):
        n = bn[b]
        t0 = b * 128
        dtA_b, dt_b = dtA[0:n, b, :], dt[0:n, b, :]
        ps, pk = B.ps()
        B.mm(ps[0:n, 0:8], [(C.U_f.ap[0:n, 0:n], dtA_b)], [C.U_f, C.lf], [pk])
        B.mm(ps[:, 8:16], [(C.ones_f.ap[0:n, :], dtA_b)], [C.ones_f, C.lf], [pk])
        B.op('dve', lambda v, ps=ps, n=n: v.tensor_copy(ssm[0:n, 0:8], ps[0:n, 0:8]), [pk], [C.ssm])
        B.op('dve', lambda v, ps=ps, n=n: v.tensor_tensor(ssm[0:n, 16:24], ps[0:n, 8:16], ssm[0:n, 0:8], ALU.subtract), [pk, C.ssm], [C.ssm])
        B.op('act', lambda a, ps=ps: a.activation(out=ssm[:, 24:32], in_=ps[:, 8:16], func=AF.Exp), [pk], [C.ssm])
        B.op('act', lambda a, n=n: a.activation(out=ssm[0:n, 16:24], in_=ssm[0:n, 16:24], func=AF.Exp), [C.ssm], [C.ssm])
        B.op('dve', lambda v, n=n, dt_b=dt_b: v.tensor_tensor(ssm[0:n, 16:24], ssm[0:n, 16:24], dt_b, ALU.mult), [C.ssm, C.lf], [C.ssm])
        B.op('dve', lambda v, n=n, dtA_b=dtA_b: v.tensor_tensor(R3[0:n, :, 0:n], dtA_b.unsqueeze(2).to_broadcast([n, 8, n]),
                                                                 C.U_f.ap[0:n, 0:n].unsqueeze(1).to_broadcast([n, 8, n]), ALU.mult),
             [C.lf, C.U_f], [C.Rb])
        pbs = []
        for X in range(2):
            pb, pbk = B.ps()
            pb3 = pb.rearrange("p (h t) -> p h t", t=128)
            if n == 128:
                B.mm(pb3[:, :, 0:n], [(C.ones_f.ap[0:n, :], R3[0:n, 4 * X:4 * X + 4, 0:n])], [C.ones_f, C.Rb], [pbk])
            else:
                for hh in range(4):
                    B.mm(pb3[:, hh, 0:n], [(C.ones_f.ap[0:n, :], R3[0:n, 4 * X + hh, 0:n])], [C.ones_f, C.Rb], [pbk])
            pbs.append((pb3, pbk))
        for X in range(2):
            pb3, pbk = pbs[X]
            B.op('act', lambda a, pb3=pb3, X=X, n=n: a.activation(out=L3[:, 4 * X:4 * X + 4, 0:n], in_=pb3[:, :, 0:n], func=AF.Exp), [pbk], [C.Lb])
        for g in range(2):
            B.op('dve', lambda v, g=g, n=n, t0=t0: v.tensor_tensor(Cs3[:, 4 * g:4 * g + 4, 0:n], L3[:, 4 * g:4 * g + 4, 0:n],
                                                                   C.xact[6 + g].ap[:, t0:t0 + n].unsqueeze(1).to_broadcast([128, 4, n]), ALU.mult),
                 [C.Lb, C.xact[6 + g]], [C.Cs])
        for X in range(2):
            pb3, pbk = pbs[X]
            B.op('dve', lambda v, pb3=pb3, X=X, n=n: v.tensor_tensor(L3[0:n, 4 * X:4 * X + 4, 0:n], pb3[0:n, :, 0:n],
                                                                     ssm[0:n, 4 * X:4 * X + 4].unsqueeze(2).to_broadcast([n, 4, n]), ALU.subtract),
                 [pbk, C.ssm, C.Lb], [C.Lb])
        B.op('dve', lambda v, n=n: v.tensor_scalar(L3[0:n, :, 0:n], L3[0:n, :, 0:n], 0.0, None, ALU.min), [C.Lb], [C.Lb])
        B.op('act', lambda a, n=n: a.activation(out=L3[0:n, :, 0:n], in_=L3[0:n, :, 0:n], func=AF.Exp), [C.Lb], [C.Lb])
        pcb, pcbk = B.ps()
        for g in range(2):
            B.mm(pcb[0:n, g * 128:g * 128 + n], [(C.xact[4 + g].ap[:, t0:t0 + n], C.xact[6 + g].ap[:, t0:t0 + n])],
                 [C.xact[4 + g], C.xact[6 + g]], [pcbk])
        pcb3 = pcb[:, 0:256].rearrange("p (g t) -> p g t", t=128)
        B.op('dve', lambda v, n=n, pcb3=pcb3: v.tensor_tensor(cbm3[0:n, :, 0:n], pcb3[0:n, :, 0:n],
                                                              C.U_f.ap[0:n, 0:n].unsqueeze(1).to_broadcast([n, 2, n]), ALU.mult), [pcbk, C.U_f], [C.cbm])
        for g in range(2):
            B.op('dve', lambda v, g=g, n=n: v.tensor_tensor(L3[0:n, 4 * g:4 * g + 4, 0:n], L3[0:n, 4 * g:4 * g + 4, 0:n],
                                                            cbm3[0:n, g, 0:n].unsqueeze(1).to_broadcast([n, 4, n]), ALU.mult), [C.Lb, C.cbm], [C.Lb])
        B.op('dve', lambda v, n=n, dt_b=dt_b: v.tensor_tensor(W3[0:n, :, 0:n], L3[0:n, :, 0:n], dt_b.unsqueeze(2).to_broadcast([n, 8, n]), ALU.mult),
             [C.Lb, C.lf], [C.Wb])
        ptx, ptxk = B.ps()
        ptxb = ptx.bitcast(BF16)
        for ci in range(4):
            B.transpose(ptxb[0:n, ci * 128:(ci + 1) * 128], C.xact[ci].ap[:, t0:t0 + n], C.identbf.ap, [C.xact[ci], C.identbf], [ptxk])
        for g in range(2):
            B.transpose(ptxb[0:n, 512 + g * 128:512 + (g + 1) * 128], C.xact[4 + g].ap[:, t0:t0 + n], C.identbf.ap, [C.xact[4 + g], C.identbf], [ptxk])
        B.op('act', lambda a, n=n, ptxb=ptxb: a.copy(C.xtm.ap[0:n, 0:512], ptxb[0:n, 0:512]), [ptxk], [C.xtm])
        B.op('act', lambda a, n=n, ptxb=ptxb: a.copy(C.Btm.ap[0:n, 0:256], ptxb[0:n, 512:768]), [ptxk], [C.Btm])
        B.op('dve', lambda v, n=n: v.tensor_tensor(C.xw.ap[0:n, 0:512].rearrange("p (h q) -> p h q", q=64),
                                                   C.xtm.ap[0:n, 0:512].rearrange("p (h q) -> p h q", q=64),
                                                   ssm[0:n, 16:24].unsqueeze(2).to_broadcast([n, 8, 64]), ALU.mult), [C.xtm, C.ssm], [C.xw])
        py, pyk = B.ps()

        def fn_y(t, n=n, py=py):
            ins = None
            for h in range(8):
                o = py[(h % 2) * 64:(h % 2) * 64 + 64, (h // 2) * 128:(h // 2) * 128 + n]
                t.matmul(o, C.xtm.ap[0:n, h * 64:(h + 1) * 64], W3[0:n, h, 0:n], start=True, stop=False)
                ins = t.matmul(o, C.STbf.ap[:, h * 64:(h + 1) * 64], Cs3[:, h, 0:n], start=False, stop=True)
            return ins
        B.op('pe', fn_y, [C.xtm, C.Wb, C.STbf, C.Cs], [pyk])
        for ci in range(4):
            B.op('dve', lambda v, ci=ci, n=n, t0=t0, py=py: v.scalar_tensor_tensor(
                C.ysT[ci].ap[:, t0:t0 + n], C.xact[ci].ap[:, t0:t0 + n], dcol(ci), py[:, ci * 128:ci * 128 + n], ALU.mult, ALU.add),
                [C.xact[ci], C.prm[l], pyk], [C.ysT[ci]])
        pst, pstk = B.ps()
        for g in range(2):
            B.mm(pst[:, g * 256:(g + 1) * 256], [(C.Btm.ap[0:n, g * 128:(g + 1) * 128], C.xw.ap[0:n, g * 256:(g + 1) * 256])], [C.Btm, C.xw], [pstk])
        B.op('dve', lambda v: v.tensor_tensor(ST3, ST3, ssm[:, 24:32].unsqueeze(2).to_broadcast([128, 8, 64]), ALU.mult), [C.ST[l], C.ssm], [C.ST[l]])
        B.op('dve', lambda v, pst=pst: v.tensor_tensor(C.ST[l].ap, C.ST[l].ap, pst[:, 0:512], ALU.add), [C.ST[l], pstk], [C.ST[l]])
        B.op('act', lambda a: a.copy(C.STbf.ap, C.ST[l].ap), [C.ST[l]], [C.STbf])
    for ci in range(4):
        B.op('dve', lambda v, ci=ci: v.tensor_tensor(C.ysT[ci].ap[:, 0:T], C.ysT[ci].ap[:, 0:T], C.zs[ci].ap[:, 0:T], ALU.mult),
             [C.ysT[ci], C.zs[ci]], [C.ysT[ci]])
        B.op('act', lambda a, ci=ci: a.activation(out=C.sq4[ci].ap[:, 0:T], in_=C.ysT[ci].ap[:, 0:T], func=AF.Square), [C.ysT[ci]], [C.sq4[ci]])
    sq_save = C.sq
    C.sq = C.sq4
    rms_rstd(B, C, None, T, 4, 512, sq_from_src=False)
    C.sq = sq_save
    for ci in range(4):
        B.op('dve', lambda v, ci=ci: v.scalar_tensor_tensor(C.mixedT[12 + ci].ap[:, 0:T], C.ysT[ci].ap[:, 0:T], nwc(ci), C.rstd.ap[:, 0:T],
                                                            ALU.mult, ALU.mult), [C.ysT[ci], C.prm[l], C.rstd], [C.mixedT[12 + ci]])
    if S.last:
        for ci in range(4):
            ps, pk = B.ps()
            B.transpose(ps[:, 0:128], C.ST[l].ap[:, ci * 128:(ci + 1) * 128], C.ident.ap, [C.ST[l], C.ident], [pk])
            B.op('dve', lambda v, ps=ps: v.tensor_copy(C.cvstage.ap[:, 0:128], ps[:, 0:128]), [pk], [C.cvstage])
            B.dma('sp', S.state_out[l][ci * 128:(ci + 1) * 128, :], C.cvstage.ap[:, 0:128], [C.cvstage], [], C.sem_cv)


def out_proj(B, C, d, l, T):
    wo = d['w_out'][l].rearrange("(c p) n -> p c n", p=128)
    for q4 in range(4):
        def loader(slot, sem, q4=q4):
            B.dma('pool', slot3(slot), wo[:, :, q4 * 512:(q4 + 1) * 512], [], [slot], sem)
        slot = B.weights(loader)
        s3 = slot3(slot)
        for j in range(4):
            c = q4 * 4 + j
            ps, pk = B.ps()
            B.mm(ps[:, 0:T], [(s3[:, n, j * 128:(j + 1) * 128], C.mixedT[n].ap[:, 0:T]) for n in range(16)], [slot] + C.mixedT, [pk])
            B.op('dve', lambda v, c=c, ps=ps: v.tensor_copy(C.yT[c].ap[:, 0:T], ps[:, 0:T]), [pk], [C.yT[c]])
            B.op('act', lambda a, c=c: a.activation(out=C.hT[c].ap[:, 0:T], in_=C.yT[c].ap[:, 0:T], func=AF.Square), [C.yT[c]], [C.hT[c]])
    sq_save = C.sq
    C.sq = C.hT
    resid_add(B, C, l, 3, T, half=False)
    C.sq = sq_save


def mixer(B, C, d, l, T, S, Treal=None):
    Treal = T if Treal is None else Treal
    nb = (T + 127) // 128
    norm_to_h(B, C, l, 2, T)
    win = d['w_in'][l].rearrange("(c p) n -> p c n", p=128)
    fox_phase(B, C, d, l, T, S, win, Treal)
    for i in range(8):
        B.dump(f"mx{i}", C.mixedT[i].ap[:, 0:T], [C.mixedT[i]])
    gm_phase(B, C, d, l, T, S, win, Treal)
    for i in range(8, 12):
        B.dump(f"mx{i}", C.mixedT[i].ap[:, 0:T], [C.mixedT[i]])
    ssd_phase(B, C, d, l, T, S, win, Treal)
    for i in range(12, 16):
        B.dump(f"mx{i}", C.mixedT[i].ap[:, 0:T], [C.mixedT[i]])
    out_proj(B, C, d, l, T)
    S.npast_c[l] += nb


def stream_begin(B, C, d, S, l, cache=None):
    sm = C.small[l]
    B.op('dve', lambda v: v.memset(sm.ap[:, 24:32], 0.0), [], [sm])
    B.op('dve', lambda v: v.memset(C.c_all[l].ap, 0.0), [], [C.c_all[l]])
    if cache is None:
        B.op('dve', lambda v: v.memset(C.ST[l].ap, 0.0), [], [C.ST[l]])
        B.op('dve', lambda v: v.memset(C.convst[l].ap, 0.0), [], [C.convst[l]])
        return
    P = cache['ck'].shape[0]
    npb = P // 128
    tmp = C.cvstage
    t3 = tmp.ap[:, 0:npb * 8].rearrange("p (k h) -> p k h", h=8)
    B.dma('sp', t3, cache['clf'].rearrange("(k p) h -> p k h", p=128), [], [tmp], C.sem_misc)
    cumsum_blocks(B, C, l, lambda b: (t3[:, b, :], [tmp]), npb, [128] * npb, 0)
    S.npast_c[l] = npb
    S.past[l] = [(cache['ck'][b * 128:(b + 1) * 128, :], cache['cv'][b * 128:(b + 1) * 128, :], (('ck', l, b), ('cv', l, b)), 128)
                 for b in range(npb)]
    cs3 = C.convst[l].ap[:, 0:24].rearrange("p (c k) -> p c k", k=3)
    st = C.kblk[0]
    for half in range(2):
        B.dma('sp', st.ap[0:3, 0:128], cache['conv'][:, 0:128], [], [st], C.sem_misc) if False else None
    cvs = C.gbuf[0]
    cvs2 = C.gbuf[1]
    B.dma('sp', cvs.ap[0:3, 0:512], cache['conv'][:, 0:512], [], [cvs], C.sem_misc)
    B.dma('sp', cvs2.ap[0:3, 0:512], cache['conv'][:, 512:1024], [], [cvs2], C.sem_misc)
    ps, pk = B.ps()
    for c in range(8):
        src = cvs if c < 4 else cvs2
        B.transpose(ps[:, c * 3:c * 3 + 3], src.ap[0:3, (c % 4) * 128:(c % 4 + 1) * 128], C.ident.ap[0:3, 0:3], [src, C.ident], [pk])
    B.op('dve', lambda v, ps=ps: v.tensor_copy(C.convst[l].ap[:, 0:24], ps[:, 0:24]), [pk], [C.convst[l]])
    ps, pk = B.ps()
    for ci in range(4):
        stg = C.wst[ci]
        B.dma('sp', stg.ap[:, 0:128], cache['state'][ci * 128:(ci + 1) * 128, :], [], [stg], C.sem_misc)
        B.transpose(ps[:, ci * 128:(ci + 1) * 128], stg.ap[:, 0:128], C.ident.ap, [stg, C.ident], [pk])
    B.op('dve', lambda v, ps=ps: v.tensor_copy(C.ST[l].ap, ps[:, 0:512]), [pk], [C.ST[l]])


L_, D_, DFF_, NIN_ = 2, 2048, 5632, 5648
SEQ_, PAST_, DECL_ = 2048, 1024, 32
TILE_ = 512


def build_full(n_tiles=SEQ_ // TILE_, do_sample=True):
    nc = bass.Bass("TRN2", target_bir_lowering=False)
    cfg = dict(D=D_, DFF=DFF_, T=TILE_, L=L_, arena_bytes=207 * 1024)
    d = {}

    def inp(name, shape):
        d[name] = nc.dram_tensor(name, shape, F32, kind="ExternalInput").ap()

    def outp(name, shape):
        d[name] = nc.dram_tensor(name, shape, F32, kind="ExternalOutput").ap()
    inp('x_p', [SEQ_, D_]); inp('x_s', [DECL_, D_])
    inp('ck', [L_, PAST_, 1024]); inp('cv', [L_, PAST_, 1024]); inp('clf', [L_, PAST_, 8])
    inp('cconv', [L_, 3, 1024]); inp('cstate', [L_, 512, 128])
    inp('norm_w', [L_, 6, D_]); inp('w_ffn_gate', [L_, 2, D_, DFF_]); inp('w_ffn_up', [L_, 2, D_, DFF_]); inp('w_ffn_down', [L_, 2, DFF_, D_])
    inp('w_in', [L_, D_, NIN_]); inp('w_out', [L_, D_, D_])
    inp('fox_fb', [L_, 8]); inp('gm_ln_w', [L_, 512]); inp('gm_ln_b', [L_, 512]); inp('gm_ws', [L_, 4, 128, 128]); inp('gm_bs', [L_, 4, 128])
    inp('ssd_conv_w', [L_, 4, 1024]); inp('ssd_conv_b', [L_, 1024]); inp('ssd_dt_bias', [L_, 8]); inp('ssd_a_log', [L_, 8])
    inp('ssd_d_rep', [L_, 512]); inp('ssd_norm_w', [L_, 512])
    inp('c_ident', [128, 128]); inp('c_tri', [128, 128])
    outp('y_p', [SEQ_, D_]); outp('p_k', [L_, SEQ_, 1024]); outp('p_v', [L_, SEQ_, 1024]); outp('p_lf', [L_, SEQ_, 8])
    outp('p_conv', [L_, 3, 1024]); outp('p_state', [L_, 512, 128])
    outp('y_s', [DECL_, D_]); outp('s_k', [L_, DECL_, 1024]); outp('s_v', [L_, DECL_, 1024]); outp('s_lf', [L_, DECL_, 8])
    outp('s_gm', [L_, DECL_, 512]); outp('s_conv', [L_, 3, 1024]); outp('s_state', [L_, 512, 128])
    with ExitStack() as stack:
        B, C = setup(nc, stack, cfg)
        setup_small(B, C, L_)
        setup_mixer(B, C)
        setup_mixer2(B, C)
        C.sem_kv = [B.new_sem(f"kv{i}") for i in range(8)]
        C.sem_lf = B.new_sem("lf")
        C.sem_kb = [B.new_sem(f"kb{i}") for i in range(6)]
        C.sem_vb = [B.new_sem(f"vb{i}") for i in range(6)]
        C.sem_wfd = B.new_sem("wfd")
        C.sem_gm = [B.new_sem(f"gm{i}") for i in range(2)]
        C.sem_cv = B.new_sem("cv")

        def program():
            load_consts(B, C, d)
            load_consts2(B, C, d)
            for l in range(L_):
                load_params(B, C, d, l)
                load_layer_small(B, C, d, l)
            Sp = Stream(L_)
            Sp.k_out = [d['p_k'][l] for l in range(L_)]
            Sp.v_out = [d['p_v'][l] for l in range(L_)]
            Sp.lf_out = [d['p_lf'][l] for l in range(L_)]
            Sp.conv_out = [d['p_conv'][l] for l in range(L_)]
            Sp.state_out = [d['p_state'][l] for l in range(L_)]
            for l in range(L_):
                stream_begin(B, C, d, Sp, l, None)
            for j in range(n_tiles):
                T = TILE_
                Sp.tok0 = j * T
                Sp.last = (j == n_tiles - 1)
                load_x_tile(B, C, d['x_p'][j * T:(j + 1) * T, :], T)
                for l in range(L_):
                    ffn(B, C, d, l, 0, T)
                    mixer(B, C, d, l, T, Sp)
                    ffn(B, C, d, l, 1, T)
                    for b in range(T // 128):
                        r0 = j * T + b * 128
                        kb = j * (T // 128) + b
                        Sp.past[l].append((d['p_k'][l][r0:r0 + 128, :], d['p_v'][l][r0:r0 + 128, :],
                                           (('kout', id(Sp), l, kb), ('vout', id(Sp), l, kb)), 128))
                store_x_tile(B, C, d['y_p'][j * T:(j + 1) * T, :], T)
            if do_sample:
                Ss = Stream(L_)
                Ss.k_out = [d['s_k'][l] for l in range(L_)]
                Ss.v_out = [d['s_v'][l] for l in range(L_)]
                Ss.lf_out = [d['s_lf'][l] for l in range(L_)]
                Ss.gm_out = [d['s_gm'][l] for l in range(L_)]
                Ss.conv_out = [d['s_conv'][l] for l in range(L_)]
                Ss.state_out = [d['s_state'][l] for l in range(L_)]
                for l in range(L_):
                    stream_begin(B, C, d, Ss, l, dict(ck=d['ck'][l], cv=d['cv'][l], clf=d['clf'][l], conv=d['cconv'][l], state=d['cstate'][l]))
                T, TR = 128, DECL_
                load_x_tile(B, C, d['x_s'], T, TR)
                for l in range(L_):
                    ffn(B, C, d, l, 0, T)
                    mixer(B, C, d, l, T, Ss, TR)
                    ffn(B, C, d, l, 1, T)
                store_x_tile(B, C, d['y_s'], TR)
        B.dry = True
        program()
        B.dry = False
        program()
        B.emit(B.dsems)
    return nc


_NC_CACHE = {}


def kernel(x_prompt, x_sample, cache_fox_k, cache_fox_v, cache_fox_logf, state_ssd_conv, state_ssd,
           norm_w, w_ffn_gate, w_ffn_up, w_ffn_down, w_in, fox_fb, gm_ln_w, gm_ln_b, gm_ws, gm_bs,
           ssd_conv_w, ssd_conv_b, ssd_dt_bias, ssd_a_log, ssd_d, ssd_norm_w, w_out):
    f32 = lambda a: np.ascontiguousarray(np.asarray(a, dtype=np.float32))
    n_cores = 8
    if 'nc' not in _NC_CACHE:
        _NC_CACHE['nc'] = build_full()
    nc = _NC_CACHE['nc']
    shared = dict(
        norm_w=f32(norm_w), w_ffn_gate=f32(w_ffn_gate), w_ffn_up=f32(w_ffn_up), w_ffn_down=f32(w_ffn_down),
        w_in=f32(w_in), w_out=f32(w_out), fox_fb=f32(fox_fb), gm_ln_w=f32(gm_ln_w), gm_ln_b=f32(gm_ln_b),
        gm_ws=f32(gm_ws), gm_bs=f32(gm_bs), ssd_conv_w=f32(ssd_conv_w), ssd_conv_b=f32(ssd_conv_b),
        ssd_dt_bias=f32(ssd_dt_bias), ssd_a_log=f32(ssd_a_log), ssd_norm_w=f32(ssd_norm_w),
        ssd_d_rep=f32(np.repeat(np.asarray(ssd_d, dtype=np.float32), 64, axis=1)),
        c_ident=np.eye(128, dtype=np.float32), c_tri=np.triu(np.ones((128, 128), np.float32)),
    )
    xp, xs = f32(x_prompt), f32(x_sample)
    ck, cv, clf = f32(cache_fox_k), f32(cache_fox_v), f32(cache_fox_logf)
    cconv, cst = f32(state_ssd_conv), f32(state_ssd)
    in_maps = []
    for c in range(n_cores):
        m = dict(shared)
        m['x_p'] = xp[c % 4]
        m['x_s'] = xs[c]
        m['ck'] = np.ascontiguousarray(ck[:, c].reshape(L_, PAST_, 1024))
        m['cv'] = np.ascontiguousarray(cv[:, c].reshape(L_, PAST_, 1024))
        m['clf'] = np.ascontiguousarray(clf[:, c])
        m['cconv'] = np.ascontiguousarray(cconv[:, c])
        m['cstate'] = np.ascontiguousarray(cst[:, c].reshape(L_, 512, 128))
        in_maps.append(m)
    res = run_bass_kernel_spmd(nc, in_maps, core_ids=list(range(n_cores)))
    r = res.results
    st = lambda key, cores: np.stack([r[c][key] for c in cores])
    P4, S8 = range(4), range(8)
    y_prompt = st('y_p', P4)
    y_sample = st('y_s', S8)
    p_k = np.transpose(st('p_k', P4), (1, 0, 2, 3)).reshape(L_, 4, SEQ_, 8, 128)
    p_v = np.transpose(st('p_v', P4), (1, 0, 2, 3)).reshape(L_, 4, SEQ_, 8, 128)
    p_lf = np.transpose(st('p_lf', P4), (1, 0, 2, 3))
    p_conv = np.transpose(st('p_conv', P4), (1, 0, 2, 3))
    p_ssm = np.transpose(st('p_state', P4), (1, 0, 2, 3)).reshape(L_, 4, 8, 64, 128)
    s_k = np.transpose(st('s_k', S8), (1, 0, 2, 3)).reshape(L_, 8, DECL_, 8, 128)
    s_v = np.transpose(st('s_v', S8), (1, 0, 2, 3)).reshape(L_, 8, DECL_, 8, 128)
    s_lf = np.transpose(st('s_lf', S8), (1, 0, 2, 3))
    s_gm = np.transpose(st('s_gm', S8), (1, 0, 2, 3))
    s_conv = np.transpose(st('s_conv', S8), (1, 0, 2, 3))
    s_ssm = np.transpose(st('s_state', S8), (1, 0, 2, 3)).reshape(L_, 8, 8, 64, 128)
    outs = (y_prompt, y_sample, p_k, p_v, p_lf, p_conv, p_ssm, s_k, s_v, s_lf, s_gm, s_conv, s_ssm)
    return tuple(np.ascontiguousarray(o, dtype=np.float32) for o in outs)
```
